# Optimizing a Trainium2 kernel written in Bass

```python
import jax
import jax.numpy as jnp
from jax import lax
import numpy as np

D_MODEL = 1024
BATCH = 8
SEQ = 4096
DEPTH = 2

CTX_LEN = 256
GRID_W = 64
EPS = 1e-6
N_MOD = 9

GROUP_W = 256
MIX_W = 4 * GROUP_W

FNET_HEADS = 4
FNET_HD = GROUP_W // FNET_HEADS

MLA_HEADS = 4
QK_NOPE = 64
QK_ROPE = 32
AXIS_ROPE = QK_ROPE // 2
V_HD = 64
Q_RANK = 192
KV_RANK = 128
ROPE_BASE = 10000.0
Q_BLOCK = 128

SGU_HEADS = 4
SGU_HD = GROUP_W // SGU_HEADS
SGU_CHUNK = 128

POOL_WINDOWS = (2, 4, 8, 16)
POOL_GROUPS = len(POOL_WINDOWS)
POOL_HD = GROUP_W // POOL_GROUPS

D_FF = 2816

OFF_F = 0
OFF_Q = OFF_F + GROUP_W
OFF_KV = OFF_Q + Q_RANK
OFF_KR = OFF_KV + KV_RANK
OFF_G = OFF_KR + QK_ROPE
OFF_P = OFF_G + 2 * GROUP_W
IN_W = OFF_P + GROUP_W

kernel_name = "hybrid_parallel_group_diffusion_block"


def rms_norm(x, g):
    xf = x.astype(jnp.float32)
    y = xf * lax.rsqrt(jnp.mean(xf * xf, axis=-1, keepdims=True) + EPS)
    return (y * g.astype(jnp.float32)).astype(x.dtype)


def modulate(x, shift, scale):
    return x * (1 + scale) + shift


def swiglu(x, w13, w2):
    a, b = jnp.split(x @ w13, 2, axis=-1)
    return (jax.nn.silu(a) * b) @ w2


def ffn_half_step(h, g, shift, scale, gate, w13, w2):
    return h + 0.5 * gate * swiglu(modulate(rms_norm(h, g), shift, scale), w13, w2)


def axial_rope_tables(n_tokens):
    rows = n_tokens // GRID_W
    row = jnp.repeat(jnp.arange(rows, dtype=jnp.float32), GRID_W)
    col = (jnp.arange(n_tokens) % GRID_W).astype(jnp.float32)
    inv = jnp.power(ROPE_BASE, -jnp.arange(0, AXIS_ROPE, 2, dtype=jnp.float32) / AXIS_ROPE)
    ang = jnp.stack([row[:, None] * inv, col[:, None] * inv], axis=1)
    return jnp.cos(ang), jnp.sin(ang)


def apply_axial_rope(x, cos, sin):
    xs = x.reshape(x.shape[:-1] + (2, 2, AXIS_ROPE // 2)).astype(jnp.float32)
    x1, x2 = xs[..., 0, :], xs[..., 1, :]
    c = cos[None, :, None]
    s = sin[None, :, None]
    out = jnp.stack([x1 * c - x2 * s, x1 * s + x2 * c], axis=-2)
    return out.reshape(x.shape).astype(x.dtype)


def fourier_mixer(z, w_f):
    B, L, _ = z.shape
    zh = z.reshape(B, L, FNET_HEADS, FNET_HD).astype(jnp.float32)
    f = jnp.fft.fft2(zh, axes=(1, 3), norm="ortho").real.astype(z.dtype)
    return jnp.einsum('blhc,hcd->blhd', f, w_f).reshape(B, L, GROUP_W)


def mla_query(cq, g_q, w_uq, rope):
    B, L, _ = cq.shape
    q = (rms_norm(cq, g_q) @ w_uq).reshape(B, L, MLA_HEADS, QK_NOPE + QK_ROPE)
    if rope is not None:
        q = jnp.concatenate([q[..., :QK_NOPE], apply_axial_rope(q[..., QK_NOPE:], *rope)], axis=-1)
    return q


def mla_keys_values(ckv, kr, g_kv, w_ukv, rope):
    B, L, _ = ckv.shape
    kv = (rms_norm(ckv, g_kv) @ w_ukv).reshape(B, L, MLA_HEADS, QK_NOPE + V_HD)
    k_rope = kr[:, :, None, :]
    if rope is not None:
        k_rope = apply_axial_rope(k_rope, *rope)
    k = jnp.concatenate([kv[..., :QK_NOPE], jnp.broadcast_to(k_rope, (B, L, MLA_HEADS, QK_ROPE))], axis=-1)
    return k, kv[..., QK_NOPE:]


def attend(q, k, v):
    s = jnp.einsum('bqhd,bkhd->bhqk', q, k).astype(jnp.float32) * (QK_NOPE + QK_ROPE) ** -0.5
    p = jax.nn.softmax(s, axis=-1).astype(v.dtype)
    return jnp.einsum('bhqk,bkhd->bqhd', p, v)


def blocked_attention(q, k, v):
    B, L, H, dk = q.shape
    qb = q.reshape(B, L // Q_BLOCK, Q_BLOCK, H, dk).transpose(1, 0, 2, 3, 4)
    ob = lax.map(lambda qi: attend(qi, k, v), qb)
    return ob.transpose(1, 0, 2, 3, 4).reshape(B, L, H, v.shape[-1])


def spatial_gating_mixer(z, g_v, w_s, b_s):
    B, L, _ = z.shape
    z = jax.nn.gelu(z)
    u, v = jnp.split(z, 2, axis=-1)
    v = rms_norm(v.reshape(B, L, SGU_HEADS, SGU_HD), g_v)
    v = v.reshape(B, L // SGU_CHUNK, SGU_CHUNK, SGU_HEADS, SGU_HD)
    v = jnp.einsum('hpq,bnqhc->bnphc', w_s, v) + b_s.T[None, None, :, :, None]
    return u * v.reshape(B, L, GROUP_W)


def pooling_mixer(z, w_p, s_p):
    B, L, _ = z.shape
    zf = z.reshape(B, L, POOL_GROUPS, POOL_HD).astype(jnp.float32)
    cs = jnp.concatenate([jnp.zeros((B, 1, POOL_GROUPS, POOL_HD), jnp.float32), jnp.cumsum(zf, axis=1)], axis=1)
    t = jnp.arange(L)[:, None]
    w = jnp.array(POOL_WINDOWS, dtype=jnp.int32)[None, :]
    lo = jnp.clip(t - w // 2, 0, L)
    hi = jnp.clip(t - w // 2 + w, 0, L)
    gi = jnp.arange(POOL_GROUPS)[None, :]
    win_sum = cs[:, hi, gi] - cs[:, lo, gi]
    pooled = win_sum / (hi - lo).astype(jnp.float32)[None, :, :, None] - zf
    y = jnp.einsum('blgc,gcd->blgd', pooled.astype(z.dtype), w_p).reshape(B, L, GROUP_W)
    return y * s_p


def mix_heads(z, k, v, rope, w_fnet, g_q, w_uq, g_sgu, w_sgu, b_sgu, w_pool, s_pool):
    B, L, _ = z.shape
    q = mla_query(z[..., OFF_Q:OFF_KV], g_q, w_uq, rope)
    att = blocked_attention(q, k, v).reshape(B, L, GROUP_W)
    return jnp.concatenate([
        fourier_mixer(z[..., OFF_F:OFF_Q], w_fnet),
        att,
        spatial_gating_mixer(z[..., OFF_G:OFF_P], g_sgu, w_sgu, b_sgu),
        pooling_mixer(z[..., OFF_P:], w_pool, s_pool),
    ], axis=-1)


def setup_inputs(seed: int = 0) -> dict:
    key = jax.random.key(seed)
    ks = iter(jax.random.split(key, 32))
    f32 = jnp.float32

    def nrm(shape, scale):
        return jax.random.normal(next(ks), shape, f32) * scale

    def gain(shape):
        return 1.0 + 0.1 * jax.random.normal(next(ks), shape, f32)

    n = DEPTH
    return {
        "x": nrm((BATCH, SEQ, D_MODEL), 1.0),
        "c": nrm((BATCH, D_MODEL), 1.0),
        "ctx": nrm((BATCH, CTX_LEN, D_MODEL), 1.0),
        "c_ctx": nrm((D_MODEL,), 1.0),
        "w_ada": nrm((n, D_MODEL, N_MOD * D_MODEL), 0.5 * D_MODEL ** -0.5),
        "b_ada": nrm((n, N_MOD * D_MODEL), 0.02),
        "g_ffn1": gain((n, D_MODEL)),
        "w13_ffn1": nrm((n, D_MODEL, 2 * D_FF), D_MODEL ** -0.5),
        "w2_ffn1": nrm((n, D_FF, D_MODEL), D_FF ** -0.5),
        "g_mix": gain((n, D_MODEL)),
        "w_in": nrm((n, D_MODEL, IN_W), D_MODEL ** -0.5),
        "w_fnet": nrm((n, FNET_HEADS, FNET_HD, FNET_HD), FNET_HD ** -0.5),
        "g_q": gain((n, Q_RANK)),
        "w_uq": nrm((n, Q_RANK, MLA_HEADS * (QK_NOPE + QK_ROPE)), Q_RANK ** -0.5),
        "g_kv": gain((n, KV_RANK)),
        "w_ukv": nrm((n, KV_RANK, MLA_HEADS * (QK_NOPE + V_HD)), KV_RANK ** -0.5),
        "g_sgu": gain((n, SGU_HEADS, SGU_HD)),
        "w_sgu": nrm((n, SGU_HEADS, SGU_CHUNK, SGU_CHUNK), SGU_CHUNK ** -0.5),
        "b_sgu": gain((n, SGU_HEADS, SGU_CHUNK)),
        "w_pool": nrm((n, POOL_GROUPS, POOL_HD, POOL_HD), POOL_HD ** -0.5),
        "s_pool": gain((n, GROUP_W)),
        "w_out": nrm((n, MIX_W, D_MODEL), MIX_W ** -0.5),
        "g_ffn2": gain((n, D_MODEL)),
        "w13_ffn2": nrm((n, D_MODEL, 2 * D_FF), D_MODEL ** -0.5),
        "w2_ffn2": nrm((n, D_FF, D_MODEL), D_FF ** -0.5),
        "g_final": gain((D_MODEL,)),
    }


def reference(x, c, ctx, c_ctx, w_ada, b_ada, g_ffn1, w13_ffn1, w2_ffn1, g_mix, w_in, w_fnet,
              g_q, w_uq, g_kv, w_ukv, g_sgu, w_sgu, b_sgu, w_pool, s_pool, w_out,
              g_ffn2, w13_ffn2, w2_ffn2, g_final):
    B, L, _ = x.shape
    rope = axial_rope_tables(L)
    silu_c = jax.nn.silu(c)
    silu_cc = jax.nn.silu(c_ctx)[None]
    h, hc = x, ctx
    for i in range(DEPTH):
        last = i == DEPTH - 1
        m = jnp.split((silu_c @ w_ada[i] + b_ada[i])[:, None, :], N_MOD, axis=-1)
        mc = jnp.split((silu_cc @ w_ada[i] + b_ada[i])[:, None, :], N_MOD, axis=-1)

        h = ffn_half_step(h, g_ffn1[i], m[0], m[1], m[2], w13_ffn1[i], w2_ffn1[i])
        hc = ffn_half_step(hc, g_ffn1[i], mc[0], mc[1], mc[2], w13_ffn1[i], w2_ffn1[i])

        n_lat = modulate(rms_norm(h, g_mix[i]), m[3], m[4])
        n_ctx = modulate(rms_norm(hc, g_mix[i]), mc[3], mc[4])
        z = n_lat @ w_in[i]
        zc = n_ctx @ (w_in[i][:, OFF_KV:OFF_G] if last else w_in[i])
        zc_kv = zc if last else zc[..., OFF_KV:OFF_G]
        kc, vc = mla_keys_values(zc_kv[..., :KV_RANK], zc_kv[..., KV_RANK:], g_kv[i], w_ukv[i], None)
        k, v = mla_keys_values(z[..., OFF_KV:OFF_KR], z[..., OFF_KR:OFF_G], g_kv[i], w_ukv[i], rope)
        mix_p = (w_fnet[i], g_q[i], w_uq[i], g_sgu[i], w_sgu[i], b_sgu[i], w_pool[i], s_pool[i])
        y = mix_heads(z, jnp.concatenate([k, kc], axis=1), jnp.concatenate([v, vc], axis=1), rope, *mix_p) @ w_out[i]
        h = h + m[5] * y

        h = ffn_half_step(h, g_ffn2[i], m[6], m[7], m[8], w13_ffn2[i], w2_ffn2[i])

        if not last:
            yc = mix_heads(zc, kc, vc, None, *mix_p) @ w_out[i]
            hc = hc + mc[5] * yc
            hc = ffn_half_step(hc, g_ffn2[i], mc[6], mc[7], mc[8], w13_ffn2[i], w2_ffn2[i])
    return rms_norm(h, g_final)
```

```python
import math
from contextlib import ExitStack

import numpy as np
import ml_dtypes

import concourse.bass as bass
import concourse.mybir as mybir
from concourse.bass_utils import run_bass_kernel_spmd

F32 = mybir.dt.float32
BF16 = mybir.dt.bfloat16
AF = mybir.ActivationFunctionType
ALU = mybir.AluOpType
AX = mybir.AxisListType

D = 1024
L = 4096
CL = 256
NT = L + CL
DFF = 2816
NJ = DFF // 128
EPS = 1e-6
SCALE = 96 ** -0.5
NVL = 613
V_GF1, V_GMIX, V_GF2, V_BADA, V_GQ0, V_GQ1, V_GKV, V_SP, V_BSB, V_GSB = 0, 8, 16, 24, 96, 97, 98, 99, 101, 357


class Ev:
    __slots__ = ("sem", "key", "val", "know")

    def __init__(self, sem, key, val, know):
        self.sem, self.key, self.val, self.know = sem, key, val, know


class Buf:
    __slots__ = ("w", "r", "multi", "name")

    def __init__(self, name="", multi=False):
        self.w, self.r, self.multi, self.name = {}, {}, multi, name


class Eng:
    def __init__(self, name, h, sem, inorder=False):
        self.name, self.h, self.sem, self.key = name, h, sem, "E_" + name
        self.cnt = 0
        self.know = {}
        self.inorder = inorder


class Queue:
    def __init__(self, eng, sems, name):
        self.eng, self.sems, self.name = eng, sems, name
        self.n = 0


class Ctx:
    def __init__(self, nc, st, nslots=12):
        self.nc = nc
        mk = lambda n: st.enter_context(nc.semaphore(n))
        self.pe = Eng("pe", nc.tensor, mk("s_pe"), inorder=True)
        self.act = Eng("act", nc.scalar, mk("s_act"))
        self.dve = Eng("dve", nc.vector, mk("s_dve"))
        self.pool = Eng("pool", nc.gpsimd, mk("s_pool"))
        self.sp = Eng("sp", nc.sync, mk("s_sp"))
        self.engs = [self.pe, self.act, self.dve, self.pool, self.sp]
        self.qs = Queue(self.sp, [mk(f"q_s{i}") for i in range(nslots)], "qs")
        self.qg = Queue(self.pool, [mk(f"q_g{i}") for i in range(nslots)], "qg")
        self.queues = [self.qs, self.qg]
        self.n_wait = 0

    def _need(self, eng, evs):
        for e in evs:
            if eng.know.get(e.key, 0) >= e.val:
                continue
            if e.key == eng.key and eng.inorder:
                continue
            eng.h.wait_ge(e.sem, e.val)
            self.n_wait += 1
            kn = eng.know
            for k, v in e.know.items():
                if kn.get(k, 0) < v:
                    kn[k] = v
            kn[e.key] = e.val

    @staticmethod
    def _deps(reads, writes):
        evs = []
        for b in reads:
            evs.extend(b.w.values())
        for b in writes:
            if not b.multi:
                evs.extend(b.w.values())
            evs.extend(b.r.values())
        return evs

    @staticmethod
    def _update(ev, reads, writes):
        for b in writes:
            if b.multi:
                b.w[ev.key] = ev
            else:
                b.w = {ev.key: ev}
                b.r = {}
        for b in reads:
            b.r[ev.key] = ev

    def op(self, eng, fns, reads=(), writes=()):
        self._need(eng, self._deps(reads, writes))
        ins = None
        for fn in fns:
            ins = fn()
        eng.cnt += 1
        ins.then_inc(eng.sem, 1)
        know = dict(eng.know)
        know[eng.key] = eng.cnt
        ev = Ev(eng.sem, eng.key, eng.cnt, know)
        self._update(ev, reads, writes)
        return ev

    def dma(self, q, fn, reads=(), writes=()):
        eng = q.eng
        self._need(eng, self._deps(reads, writes))
        ns = len(q.sems)
        slot, rnd = q.n % ns, q.n // ns
        key = f"{q.name}{slot}"
        sem = q.sems[slot]
        if rnd > 0 and eng.know.get(key, 0) < 16 * rnd:
            eng.h.wait_ge(sem, 16 * rnd)
            self.n_wait += 1
            eng.know[key] = 16 * rnd
        ins = fn(eng.h)
        ins.then_inc(sem, 16)
        q.n += 1
        ev = Ev(sem, key, 16 * (rnd + 1), dict(eng.know))
        self._update(ev, reads, writes)
        return ev

    def all_events(self):
        evs = []
        for e in self.engs:
            if e.cnt > 0:
                evs.append(Ev(e.sem, e.key, e.cnt, {}))
        for q in self.queues:
            ns = len(q.sems)
            for s in range(min(ns, q.n)):
                last = ((q.n - 1 - s) // ns) * ns + s
                evs.append(Ev(q.sems[s], f"{q.name}{s}", 16 * (last // ns + 1), {}))
        return evs

    def barrier(self, engs=None):
        evs = self.all_events()
        for e in (engs or self.engs):
            self._need(e, evs)


class Prog:
    def __init__(self, cfg):
        self.cfg = cfg
        self.nc = bass.Bass("TRN2", target_bir_lowering=False)
        self.uid = 0

    def din(self, name, shape, dt=F32):
        return self.nc.dram_tensor(name, list(shape), dt, kind="ExternalInput").ap()

    def dscr(self, name, shape, dt):
        kind = "ExternalOutput" if name in self.cfg.get("debug", ()) else "Internal"
        return self.nc.dram_tensor(name, list(shape), dt, kind=kind).ap()

    def sb(self, st, name, shape, dt, multi=False):
        self.uid += 1
        t = st.enter_context(self.nc.sbuf_tensor(f"{name}_{self.uid}", list(shape), dt))
        return t, Buf(name, multi)

    def build(self):
        nc, cfg = self.nc, self.cfg
        nl = cfg.get("n_layers", 2)
        I = {}
        I["x"] = self.din("x", [L, D])
        I["ctx"] = self.din("ctx", [CL, D])
        I["cc"] = self.din("cc", [128, 16])
        I["wada"] = self.din("wada", [2, D, 9 * D])
        I["w13a"] = self.din("w13a", [2, D, 2 * DFF])
        I["w2a"] = self.din("w2a", [2, DFF, D])
        I["w13b"] = self.din("w13b", [2, D, 2 * DFF])
        I["w2b"] = self.din("w2b", [2, DFF, D])
        I["winx"] = self.din("winx", [2, D, 1536])
        I["wuq"] = self.din("wuq", [2, 192, 384])
        I["wuqp"] = self.din("wuqp", [2, 192, 384])
        I["wukv"] = self.din("wukv", [2, 128, 512])
        I["wfn"] = self.din("wfn", [2, 64, 4 * 64])
        I["wsT"] = self.din("wsT", [2, 128, 4 * 128])
        I["wpl"] = self.din("wpl", [2, 128, 2 * 64])
        I["wout"] = self.din("wout", [2, D, D])
        I["vecs"] = self.din("vecs", [128, 2 * NVL + 8])
        I["ident"] = self.din("ident", [128, 128])
        I["c64"] = self.din("c64", [64, 4 * 2 * 128])
        I["rope"] = self.din("rope", [128, 2 * NT])
        I["bands"] = self.din("bands", [128, 4 * 5 * 128], BF16)
        I["fw1"] = self.din("fw1", [128, 128], BF16)
        I["fw1c"] = self.din("fw1c", [8, 8], BF16)
        I["fm3"] = self.din("fm3", [128, 64 * 64], BF16)
        I["fm3c"] = self.din("fm3c", [128, 4 * 64], BF16)
        self.I = I
        self.out = nc.dram_tensor("out", [L, D], F32, kind="ExternalOutput").ap()
        S = {}
        S["hT"] = self.dscr("hT", [D, NT], F32)
        S["UVd"] = self.dscr("UVd", [NT, 512], BF16)
        S["QTd"] = self.dscr("QTd", [96, 4 * NT], BF16)
        S["KTd"] = self.dscr("KTd", [96, 4 * NT], BF16)
        S["Vd"] = self.dscr("Vd", [NT, 256], BF16)
        S["uTd"] = self.dscr("uTd", [256, NT], BF16)
        S["vd"] = self.dscr("vd", [NT, 256], BF16)
        S["pd"] = self.dscr("pd", [NT, 256], BF16)
        S["mixT"] = self.dscr("mixT", [D, NT], BF16)
        S["Ad"] = self.dscr("Ad", [2 * 64 * 64 * 256], BF16)
        self.S = S
        self.B = {k: Buf(k, multi=True) for k in S}
        self.B["out"] = Buf("out", multi=True)

        with ExitStack() as top:
            K = self.K = Ctx(nc, top)
            self.ps = []
            self.psb = []
            self.psall = top.enter_context(nc.psum_tensor("psall", [128, 8 * 512], F32))
            for i in range(8):
                self.ps.append(self.psall[:, i * 512:(i + 1) * 512])
                self.psb.append(Buf(f"ps{i}"))
            self.vecs, self.b_vecs = self.sb(top, "vecs", [128, 2 * NVL + 8], F32)
            self.ident, self.b_ident = self.sb(top, "ident", [128, 128], F32)
            self.ones, self.b_ones = self.sb(top, "ones", [128, 128], BF16)
            self.modv, self.b_modv = self.sb(top, "modv", [128, 2 * 2 * 72], F32)
            self.AB, self.b_AB = self.sb(top, "AB", [128, 2 * 3 * 2 * 3 * 8], F32)
            K.dma(K.qs, lambda e: e.dma_start(out=self.vecs[:, :], in_=I["vecs"][:, :]), writes=[self.b_vecs])
            K.dma(K.qs, lambda e: e.dma_start(out=self.ident[:, :], in_=I["ident"][:, :]), writes=[self.b_ident])
            K.op(K.dve, [lambda: nc.vector.memset(self.ones[:, :], 1.0)], writes=[self.b_ones])

            stages = cfg.get("stages", None)

            def want(name):
                return stages is None or name in stages

            if want("pro"):
                self.stage_transpose_in()
                self.stage_ada(nl)
            for l in range(nl):
                last = l == 1
                if want(f"ffn1_{l}"):
                    self.stage_ffn(l, 0, with_ctx=True, pre_mix=False)
                if want(f"proj_{l}"):
                    self.stage_proj(l)
                if want(f"att_{l}") or want(f"dft_{l}"):
                    self.stage_attdft(l)
                if want(f"sgp_{l}"):
                    self.stage_sgp(l)
                if want(f"ffn2_{l}"):
                    self.stage_ffn(l, 2, with_ctx=not last, pre_mix=True)
            if want("fin"):
                self.stage_final()
            K.barrier()
        return nc

    def ab(self, l, sub, which, kind):
        o = (((l * 3 + sub) * 2 + which) * 3 + kind) * 8
        return self.AB[:, o:o + 8]

    def vcol(self, l, off, n=1):
        o = l * NVL + off
        return self.vecs[:, o:o + n]

    def stage_transpose_in(self):
        nc, K, I = self.nc, self.K, self.I
        hTv = self.S["hT"].rearrange("(c p) t -> p c t", p=128)
        with ExitStack() as st:
            xin = [self.sb(st, f"xin{i}", [128, 4, D], F32) for i in range(2)]
            xT = [self.sb(st, f"xT{i}", [128, 8, 512], F32) for i in range(2)]
            groups = [(I["x"], g * 512, 4, g * 512) for g in range(8)] + [(I["ctx"], 0, 2, L)]
            for gi, (src, r0, nk, c0) in enumerate(groups):
                xi, bxi = xin[gi % 2]
                xo, bxo = xT[gi % 2]
                srcv = src[r0:r0 + nk * 128, :].rearrange("(k p) d -> p k d", p=128)
                K.dma(K.qs, lambda e, xi=xi, srcv=srcv, nk=nk: e.dma_start(out=xi[:, 0:nk, :], in_=srcv), writes=[bxi])
                for c in range(8):
                    pb = c
                    fns = [
                        (lambda k=k, c=c, pb=pb, xi=xi: nc.tensor.transpose(self.ps[pb][:, k * 128:(k + 1) * 128], xi[:, k, c * 128:(c + 1) * 128], self.ident[:, :]))
                        for k in range(nk)
                    ]
                    K.op(K.pe, fns, reads=[bxi, self.b_ident], writes=[self.psb[pb]])
                    if c % 2 == 0:
                        K.op(K.act, [lambda c=c, pb=pb, xo=xo, nk=nk: nc.scalar.copy(out=xo[:, c, 0:nk * 128], in_=self.ps[pb][:, 0:nk * 128])],
                             reads=[self.psb[pb]], writes=[bxo])
                    else:
                        K.op(K.dve, [lambda c=c, pb=pb, xo=xo, nk=nk: nc.vector.tensor_copy(out=xo[:, c, 0:nk * 128], in_=self.ps[pb][:, 0:nk * 128])],
                             reads=[self.psb[pb]], writes=[bxo])
                K.dma(K.qs, lambda e, xo=xo, c0=c0, nk=nk: e.dma_start(out=hTv[:, :, c0:c0 + nk * 128], in_=xo[:, :, 0:nk * 128]),
                      reads=[bxo], writes=[self.B["hT"]])
            K.barrier()

    def stage_ada(self, nl):
        nc, K, I = self.nc, self.K, self.I
        with ExitStack() as st:
            cc, bcc = self.sb(st, "cc", [128, 16], F32)
            sc, bsc = self.sb(st, "sc", [128, 16], F32)
            wa = [self.sb(st, f"wa{i}", [128, 8, 512], F32) for i in range(3)]
            mrow, bmrow = self.sb(st, "mrow", [2, 9 * D], F32)
            K.dma(K.qs, lambda e: e.dma_start(out=cc[:, :], in_=I["cc"][:, :]), writes=[bcc])
            K.op(K.act, [lambda: nc.scalar.activation(out=sc[:, :], in_=cc[:, :], func=AF.Silu)], reads=[bcc], writes=[bsc])
            scv = sc[:, :].rearrange("p (k w) -> p k w", w=2)
            for l in range(nl):
                wv = I["wada"][l].rearrange("(k p) n -> p k n", p=128)
                pb = 6 + l
                for grp in range(18):
                    w, bw = wa[grp % 3]
                    K.dma(K.qs, lambda e, w=w, grp=grp, wv=wv: e.dma_start(out=w[:, :, :], in_=wv[:, :, grp * 512:(grp + 1) * 512]), writes=[bw])
                    rb = grp % 4
                    fns = [(lambda k=k, w=w, rb=rb: nc.tensor.matmul(self.ps[rb][0:2, :], lhsT=scv[:, k, :], rhs=w[:, k, :], start=(k == 0), stop=(k == 7))) for k in range(8)]
                    K.op(K.pe, fns, reads=[bw, bsc], writes=[self.psb[rb]])
                    K.op(K.act, [lambda grp=grp, rb=rb: nc.scalar.copy(out=mrow[0:2, grp * 512:(grp + 1) * 512], in_=self.ps[rb][0:2, :])], reads=[self.psb[rb]], writes=[bmrow])
                fns = [(lambda c=c, pb=pb: nc.tensor.transpose(self.ps[pb][:, 2 * c:2 * c + 2], mrow[0:2, c * 128:(c + 1) * 128], self.ident[0:2, 0:2])) for c in range(72)]
                K.op(K.pe, fns, reads=[bmrow, self.b_ident], writes=[self.psb[pb]])
                psv = self.ps[pb][:, 0:144].rearrange("p (c w) -> p c w", w=2)
                for wch in range(2):
                    o = (l * 2 + wch) * 72
                    K.op(K.dve, [lambda o=o, wch=wch, psv=psv, l=l: nc.vector.tensor_tensor(
                        out=self.modv[:, o:o + 72], in0=psv[:, :, wch], in1=self.vcol(l, V_BADA, 72), op=ALU.add)],
                        reads=[self.psb[pb], self.b_vecs], writes=[self.b_modv])
                for sub, (gcol, gate_mul) in enumerate(((V_GF1, 0.5), (V_GMIX, 1.0), (V_GF2, 0.5))):
                    for wch in range(2):
                        o = (l * 2 + wch) * 72
                        sh = self.modv[:, o + (3 * sub) * 8: o + (3 * sub) * 8 + 8]
                        scl = self.modv[:, o + (3 * sub + 1) * 8: o + (3 * sub + 1) * 8 + 8]
                        gt = self.modv[:, o + (3 * sub + 2) * 8: o + (3 * sub + 2) * 8 + 8]
                        K.op(K.dve, [lambda scl=scl, l=l, sub=sub, wch=wch, gcol=gcol: nc.vector.scalar_tensor_tensor(
                            out=self.ab(l, sub, wch, 0), in0=scl, scalar=1.0, in1=self.vcol(l, gcol, 8), op0=ALU.add, op1=ALU.mult)],
                            reads=[self.b_modv, self.b_vecs], writes=[self.b_AB])
                        K.op(K.dve, [lambda sh=sh, l=l, sub=sub, wch=wch: nc.vector.tensor_copy(out=self.ab(l, sub, wch, 1), in_=sh)],
                             reads=[self.b_modv], writes=[self.b_AB])
                        K.op(K.dve, [lambda gt=gt, l=l, sub=sub, wch=wch, gate_mul=gate_mul: nc.vector.tensor_scalar(
                            out=self.ab(l, sub, wch, 2), in0=gt, scalar1=gate_mul, scalar2=None, op0=ALU.mult)],
                            reads=[self.b_modv], writes=[self.b_AB])
            K.barrier()

    def rms_modulate(self, xt, bxt, xn, bxn, rs, brs, tmps, segs, subs, l, sub, banks):
        nc, K = self.nc, self.K
        T = sum(w for _, w, _ in segs)
        K.op(K.act, [lambda: nc.scalar.activation(out=xn[:, :, 0:T], in_=xt[:, :, 0:T], func=AF.Square)], reads=[bxt], writes=[bxn])
        for si, (c0, w) in enumerate(subs):
            pb = banks[si]
            fns = [(lambda k=k, pb=pb, c0=c0, w=w: nc.tensor.matmul(self.ps[pb][:, 0:w], lhsT=self.ones[:, :], rhs=xn[:, k, c0:c0 + w], start=(k == 0), stop=(k == 7)))
                   for k in range(8)]
            K.op(K.pe, fns, reads=[bxn, self.b_ones], writes=[self.psb[pb]])
            K.op(K.act, [lambda pb=pb, c0=c0, w=w: nc.scalar.activation(out=rs[:, c0:c0 + w], in_=self.ps[pb][:, 0:w], func=AF.Sqrt, bias=self.epsb[:, 0:1], scale=1.0 / D)],
                 reads=[self.psb[pb], self.b_eps], writes=[brs])
        K.op(K.dve, [lambda: nc.vector.reciprocal(out=rs[:, 0:T], in_=rs[:, 0:T])], reads=[brs], writes=[brs])
        for c in range(8):
            tm, btm = tmps[c % len(tmps)]
            for (c0, w, wch) in segs:
                K.op(K.dve, [lambda c=c, c0=c0, w=w, wch=wch, tm=tm: nc.vector.scalar_tensor_tensor(
                    out=tm[:, c0:c0 + w], in0=xt[:, c, c0:c0 + w], scalar=self.ab(l, sub, wch, 0)[:, c:c + 1], in1=rs[:, c0:c0 + w],
                    op0=ALU.mult, op1=ALU.mult)], reads=[bxt, brs, self.b_AB], writes=[btm])
            for (c0, w, wch) in segs:
                K.op(K.act, [lambda c=c, c0=c0, w=w, wch=wch, tm=tm: nc.scalar.activation(
                    out=xn[:, c, c0:c0 + w], in_=tm[:, c0:c0 + w], func=AF.Identity, bias=self.ab(l, sub, wch, 1)[:, c:c + 1], scale=1.0)],
                    reads=[btm, self.b_AB], writes=[bxn])

    def ensure_eps(self, st):
        nc, K = self.nc, self.K
        self.epsb, self.b_eps = self.sb(st, "eps", [128, 1], F32)
        K.op(K.dve, [lambda: nc.vector.memset(self.epsb[:, :], EPS)], writes=[self.b_eps])

    def stage_ffn(self, l, sub, with_ctx, pre_mix):
        nc, K, I, S = self.nc, self.K, self.I, self.S
        T = 1088 if with_ctx else 1024
        segs = [(0, 1024, 0)] + ([(1024, 64, 1)] if with_ctx else [])
        subs = [(0, 512), (512, 512)] + ([(1024, 64)] if with_ctx else [])
        nS = len(subs)
        w13 = (I["w13a"] if sub == 0 else I["w13b"])[l].rearrange("(k p) n -> p k n", p=128)
        w2 = (I["w2a"] if sub == 0 else I["w2b"])[l].rearrange("(j p) d -> p j d", p=128)
        wov = I["wout"][l].rearrange("(k p) n -> p k n", p=128)
        hTv = S["hT"].rearrange("(c p) t -> p c t", p=128)
        mixv = S["mixT"].rearrange("(c p) t -> p c t", p=128)
        ps, psb = self.ps, self.psb
        with ExitStack() as st:
            self.ensure_eps(st)
            xts = [self.sb(st, f"xt{i}", [128, 8, T], F32) for i in range(2)]
            xn, bxn = self.sb(st, "xn", [128, 8, T], BF16)
            rs, brs = self.sb(st, "rs", [128, T], F32)
            tm, btm = self.sb(st, "tmp", [128, T], F32)
            sas = [self.sb(st, f"sa{i}", [128, T], BF16) for i in range(2)]
            g, bg = self.sb(st, "g", [128, NJ, T], BF16)
            wab = [(self.sb(st, f"w13a{i}", [128, 8, 256], BF16), self.sb(st, f"w13b{i}", [128, 8, 256], BF16)) for i in range(2)]
            w2r = [self.sb(st, f"w2r{i}", [128, NJ, 256], BF16) for i in range(2)]
            wor = [self.sb(st, f"wor{i}", [128, 8, 256], BF16) for i in range(2)] if pre_mix else None
            Abanks, Bbanks = (0, 1, 2), (3, 4, 5)
            cnt = {"w13": 0, "w2": 0, "wo": 0}

            def load(t):
                xt, bxt = xts[t % 2]
                K.dma(K.qs, lambda e: e.dma_start(out=xt[:, :, 0:1024], in_=hTv[:, :, t * 1024:(t + 1) * 1024]), reads=[self.B["hT"]], writes=[bxt])
                if with_ctx:
                    K.dma(K.qs, lambda e: e.dma_start(out=xt[:, :, 1024:1088], in_=hTv[:, :, L + t * 64:L + (t + 1) * 64]), reads=[self.B["hT"]], writes=[bxt])

            def premix(t):
                xt, bxt = xts[t % 2]
                K.dma(K.qs, lambda e: e.dma_start(out=xn[:, :, 0:1024], in_=mixv[:, :, t * 1024:(t + 1) * 1024]), reads=[self.B["mixT"]], writes=[bxn])
                if with_ctx:
                    K.dma(K.qs, lambda e: e.dma_start(out=xn[:, :, 1024:1088], in_=mixv[:, :, L + t * 64:L + (t + 1) * 64]), reads=[self.B["mixT"]], writes=[bxn])
                for i2 in range(4):
                    wo, bwo = wor[cnt["wo"] % 2]
                    cnt["wo"] += 1
                    K.dma(K.qg, lambda e, wo=wo, i2=i2: e.dma_start(out=wo[:, :, :], in_=wov[:, :, i2 * 256:(i2 + 1) * 256]), writes=[bwo])
                    for ii in range(2):
                        i = i2 * 2 + ii
                        banks = Abanks if i % 2 == 0 else Bbanks
                        fns = []
                        for k in range(8):
                            for si, (c0, w) in enumerate(subs):
                                fns.append(lambda ii=ii, k=k, c0=c0, w=w, pb=banks[si], wo=wo: nc.tensor.matmul(
                                    ps[pb][:, 0:w], lhsT=wo[:, k, ii * 128:(ii + 1) * 128], rhs=xn[:, k, c0:c0 + w], start=(k == 0), stop=(k == 7)))
                        K.op(K.pe, fns, reads=[bwo, bxn], writes=[psb[b_] for b_ in banks[:nS]])
                        for si, (c0, w) in enumerate(subs):
                            wch = 0 if c0 < 1024 else 1
                            K.op(K.dve, [lambda i=i, c0=c0, w=w, wch=wch, pb=banks[si]: nc.vector.scalar_tensor_tensor(
                                out=xt[:, i, c0:c0 + w], in0=ps[pb][:, 0:w], scalar=self.ab(l, 1, wch, 2)[:, i:i + 1], in1=xt[:, i, c0:c0 + w],
                                op0=ALU.mult, op1=ALU.add)], reads=[psb[banks[si]], self.b_AB, bxt], writes=[bxt])

            def norm1(t):
                xt, bxt = xts[t % 2]
                K.op(K.act, [lambda: nc.scalar.activation(out=xn[:, :, 0:T], in_=xt[:, :, 0:T], func=AF.Square)], reads=[bxt], writes=[bxn])
                nbanks = (6, 7, 6)
                for si, (c0, w) in enumerate(subs):
                    pb = nbanks[si]
                    fns = [(lambda k=k, pb=pb, c0=c0, w=w: nc.tensor.matmul(ps[pb][:, 0:w], lhsT=self.ones[:, :], rhs=xn[:, k, c0:c0 + w], start=(k == 0), stop=(k == 7)))
                           for k in range(8)]
                    K.op(K.pe, fns, reads=[bxn, self.b_ones], writes=[psb[pb]])
                    K.op(K.act, [lambda pb=pb, c0=c0, w=w: nc.scalar.activation(out=rs[:, c0:c0 + w], in_=ps[pb][:, 0:w], func=AF.Sqrt, bias=self.epsb[:, 0:1], scale=1.0 / D)],
                         reads=[psb[pb], self.b_eps], writes=[brs])
                K.op(K.dve, [lambda: nc.vector.reciprocal(out=rs[:, 0:T], in_=rs[:, 0:T])], reads=[brs], writes=[brs])

            def norm2(t):
                xt, bxt = xts[t % 2]
                for c in range(8):
                    for (c0, w, wch) in segs:
                        K.op(K.dve, [lambda c=c, c0=c0, w=w, wch=wch: nc.vector.scalar_tensor_tensor(
                            out=tm[:, c0:c0 + w], in0=xt[:, c, c0:c0 + w], scalar=self.ab(l, sub, wch, 0)[:, c:c + 1], in1=rs[:, c0:c0 + w],
                            op0=ALU.mult, op1=ALU.mult)], reads=[bxt, brs, self.b_AB], writes=[btm])
                    for (c0, w, wch) in segs:
                        K.op(K.act, [lambda c=c, c0=c0, w=w, wch=wch: nc.scalar.activation(
                            out=xn[:, c, c0:c0 + w], in_=tm[:, c0:c0 + w], func=AF.Identity, bias=self.ab(l, sub, wch, 1)[:, c:c + 1], scale=1.0)],
                            reads=[btm, self.b_AB], writes=[bxn])

            def phase1(t):
                for gi in range(NJ // 2):
                    j0 = gi * 2
                    (wa, bwa), (wb, bwb) = wab[cnt["w13"] % 2]
                    cnt["w13"] += 1
                    K.dma(K.qg, lambda e, wa=wa, j0=j0: e.dma_start(out=wa[:, :, :], in_=w13[:, :, j0 * 128:(j0 + 2) * 128]), writes=[bwa])
                    K.dma(K.qg, lambda e, wb=wb, j0=j0: e.dma_start(out=wb[:, :, :], in_=w13[:, :, DFF + j0 * 128:DFF + (j0 + 2) * 128]), writes=[bwb])
                    for jj in range(2):
                        j = j0 + jj
                        sa, bsa = sas[j % 2]
                        fns = []
                        for k in range(8):
                            for si, (c0, w) in enumerate(subs):
                                fns.append(lambda jj=jj, k=k, c0=c0, w=w, pb=Abanks[si], wa=wa: nc.tensor.matmul(
                                    ps[pb][:, 0:w], lhsT=wa[:, k, jj * 128:(jj + 1) * 128], rhs=xn[:, k, c0:c0 + w], start=(k == 0), stop=(k == 7)))
                        K.op(K.pe, fns, reads=[bwa, bxn], writes=[psb[b_] for b_ in Abanks[:nS]])
                        for si, (c0, w) in enumerate(subs):
                            K.op(K.act, [lambda c0=c0, w=w, pb=Abanks[si], sa=sa: nc.scalar.activation(out=sa[:, c0:c0 + w], in_=ps[pb][:, 0:w], func=AF.Silu)],
                                 reads=[psb[Abanks[si]]], writes=[bsa])
                        fns = []
                        for k in range(8):
                            for si, (c0, w) in enumerate(subs):
                                fns.append(lambda jj=jj, k=k, c0=c0, w=w, pb=Bbanks[si], wb=wb: nc.tensor.matmul(
                                    ps[pb][:, 0:w], lhsT=wb[:, k, jj * 128:(jj + 1) * 128], rhs=xn[:, k, c0:c0 + w], start=(k == 0), stop=(k == 7)))
                        K.op(K.pe, fns, reads=[bwb, bxn], writes=[psb[b_] for b_ in Bbanks[:nS]])
                        for si, (c0, w) in enumerate(subs):
                            K.op(K.dve, [lambda j=j, c0=c0, w=w, pb=Bbanks[si], sa=sa: nc.vector.tensor_tensor(
                                out=g[:, j, c0:c0 + w], in0=ps[pb][:, 0:w], in1=sa[:, c0:c0 + w], op=ALU.mult)],
                                reads=[psb[Bbanks[si]], bsa], writes=[bg])

            def phase2(t, i2s):
                xt, bxt = xts[t % 2]
                for i2 in i2s:
                    wr, bwr = w2r[cnt["w2"] % 2]
                    cnt["w2"] += 1
                    K.dma(K.qg, lambda e, wr=wr, i2=i2: e.dma_start(out=wr[:, :, :], in_=w2[:, :, i2 * 256:(i2 + 1) * 256]), writes=[bwr])
                    for ii in range(2):
                        i = i2 * 2 + ii
                        banks = Abanks if i % 2 == 0 else Bbanks
                        fns = []
                        for j in range(NJ):
                            for si, (c0, w) in enumerate(subs):
                                fns.append(lambda ii=ii, j=j, c0=c0, w=w, pb=banks[si], wr=wr: nc.tensor.matmul(
                                    ps[pb][:, 0:w], lhsT=wr[:, j, ii * 128:(ii + 1) * 128], rhs=g[:, j, c0:c0 + w], start=(j == 0), stop=(j == NJ - 1)))
                        K.op(K.pe, fns, reads=[bwr, bg], writes=[psb[b_] for b_ in banks[:nS]])
                        for si, (c0, w) in enumerate(subs):
                            wch = 0 if c0 < 1024 else 1
                            K.op(K.dve, [lambda i=i, c0=c0, w=w, wch=wch, pb=banks[si]: nc.vector.scalar_tensor_tensor(
                                out=xt[:, i, c0:c0 + w], in0=ps[pb][:, 0:w], scalar=self.ab(l, sub, wch, 2)[:, i:i + 1], in1=xt[:, i, c0:c0 + w],
                                op0=ALU.mult, op1=ALU.add)], reads=[psb[banks[si]], self.b_AB, bxt], writes=[bxt])

            def store(t):
                xt, bxt = xts[t % 2]
                K.dma(K.qs, lambda e: e.dma_start(out=hTv[:, :, t * 1024:(t + 1) * 1024], in_=xt[:, :, 0:1024]), reads=[bxt], writes=[self.B["hT"]])
                if with_ctx:
                    K.dma(K.qs, lambda e: e.dma_start(out=hTv[:, :, L + t * 64:L + (t + 1) * 64], in_=xt[:, :, 1024:1088]), reads=[bxt], writes=[self.B["hT"]])

            load(0)
            if pre_mix:
                premix(0)
            norm1(0)
            norm2(0)
            for t in range(4):
                nxt = t + 1 < 4
                if nxt:
                    load(t + 1)
                phase1(t)
                phase2(t, [0])
                if nxt:
                    if pre_mix:
                        premix(t + 1)
                    norm1(t + 1)
                phase2(t, [1])
                if nxt:
                    norm2(t + 1)
                phase2(t, [2, 3])
                store(t)
            K.barrier()

    def stage_proj(self, l):
        nc, K, I, S = self.nc, self.K, self.I, self.S
        hTv = S["hT"].rearrange("(c p) t -> p c t", p=128)
        ropev = I["rope"].rearrange("p (w t) -> p w t", w=2)
        QTv = S["QTd"].rearrange("r (h t) -> r h t", h=4)
        KTv = S["KTd"].rearrange("r (h t) -> r h t", h=4)
        uTv = S["uTd"].rearrange("(c p) t -> p c t", p=128)
        with ExitStack() as st:
            self.ensure_eps(st)
            win, bwin = self.sb(st, "win", [128, 8, 1536], BF16)
            wuq, bwuq = self.sb(st, "wuq", [128, 2, 384], BF16)
            wuqp, bwuqp = self.sb(st, "wuqp", [128, 2, 384], BF16)
            wukv, bwukv = self.sb(st, "wukv", [128, 512], BF16)
            wv, bwv = self.sb(st, "wv", [128, 4, 64], BF16)
            wfn, bwfn = self.sb(st, "wfn", [64, 256], F32)
            c64, bc64 = self.sb(st, "c64", [64, 1024], F32)
            csw, bcsw = self.sb(st, "csw", [128, 2, 2, 512], BF16)
            xts = [self.sb(st, f"pxt{i}", [128, 8, 512], F32) for i in range(2)]
            rts = [self.sb(st, f"prt{i}", [128, 2, 512], F32) for i in range(2)]
            xns = [self.sb(st, f"pxn{i}", [128, 8, 512], BF16) for i in range(2)]
            rs, brs = self.sb(st, "prs", [128, 512], F32)
            tmps = [self.sb(st, f"ptmp{i}", [128, 512], F32) for i in range(2)]
            zF, bzF = self.sb(st, "zF", [128, 2, 512], BF16)
            UVt, bUVt = self.sb(st, "UVt", [128, 4, 512], BF16)
            sqa, bsqa = self.sb(st, "sqa", [128, 512], BF16)
            sqb, bsqb = self.sb(st, "sqb", [128, 512], BF16)
            sqc, bsqc = self.sb(st, "sqc", [128, 512], BF16)
            rq, brq = self.sb(st, "rq", [128, 512], F32)
            rkv, brkv = self.sb(st, "rkv", [128, 512], F32)
            cqa, bcqa = self.sb(st, "cqa", [128, 512], BF16)
            cqb, bcqb = self.sb(st, "cqb", [128, 512], BF16)
            ckvn, bckvn = self.sb(st, "ckvn", [128, 512], BF16)
            t1, bt1 = self.sb(st, "t1", [128, 512], F32)
            t2, bt2 = self.sb(st, "t2", [128, 512], F32)
            QTt, bQTt = self.sb(st, "QTt", [128, 4, 512], BF16)
            KTt, bKTt = self.sb(st, "KTt", [128, 4, 512], BF16)
            Vt, bVt = self.sb(st, "Vt", [128, 4, 256], BF16)
            uTt, buTt = self.sb(st, "uTt", [128, 2, 512], BF16)
            v_t, bv_t = self.sb(st, "v_t", [128, 4, 256], BF16)
            p_t, bp_t = self.sb(st, "p_t", [128, 4, 256], BF16)
            gv, bgv = self.sb(st, "gv", [128, 4, 256], F32)
            sqv, bsqv = self.sb(st, "sqv", [128, 4, 256], F32)
            ssv, bssv = self.sb(st, "ssv", [128, 16], F32)
            ps, psb = self.ps, self.psb
            K.dma(K.qg, lambda e: e.dma_start(out=win[:, :, :], in_=I["winx"][l].rearrange("(k p) n -> p k n", p=128)), writes=[bwin])
            for (dst, bdst, src) in ((wuq, bwuq, I["wuq"]), (wuqp, bwuqp, I["wuqp"])):
                K.dma(K.qg, lambda e, dst=dst, src=src: e.dma_start(out=dst[:, 0, :], in_=src[l, 0:128, :]), writes=[bdst])
                K.dma(K.qg, lambda e, dst=dst, src=src: e.dma_start(out=dst[0:64, 1, :], in_=src[l, 128:192, :]), writes=[bdst])
            K.dma(K.qg, lambda e: e.dma_start(out=wukv[:, :], in_=I["wukv"][l]), writes=[bwukv])
            K.dma(K.qg, lambda e: e.dma_start(out=wv[:, :, :], in_=I["wukv"][l].rearrange("r (h two c) -> r h two c", two=2, c=64)[:, :, 1, :]), writes=[bwv])
            K.dma(K.qs, lambda e: e.dma_start(out=wfn[:, :], in_=I["wfn"][l]), writes=[bwfn])
            K.dma(K.qs, lambda e: e.dma_start(out=c64[:, :], in_=I["c64"][:, :]), writes=[bc64])
            K.op(K.dve, [lambda: nc.vector.memset(csw[:, :, :, :], 0.0)], writes=[bcsw])
            for cq_ in range(2):
                for lc in range(2):
                    pb = cq_ * 2 + lc
                    fns = []
                    for part, kind in ((0, 2 * lc), (1, 2 * lc + 1)):
                        for hh in range(2):
                            h = 2 * cq_ + hh
                            co = part * 256 + h * 64
                            ko = (kind * 2 + hh) * 128
                            fns.append(lambda co=co, ko=ko, h=h, pb=pb: nc.tensor.matmul(
                                ps[pb][:, co:co + 64], lhsT=c64[:, ko:ko + 128], rhs=wfn[:, h * 64:(h + 1) * 64], start=True, stop=True))
                    K.op(K.pe, fns, reads=[bc64, bwfn], writes=[psb[pb]])
                    for part in range(2):
                        co = part * 256 + cq_ * 128
                        K.op(K.dve, [lambda co=co, pb=pb, cq_=cq_, lc=lc: nc.vector.tensor_copy(out=csw[:, cq_, lc, co:co + 128], in_=ps[pb][:, co:co + 128])],
                             reads=[psb[pb]], writes=[bcsw])
            tiles = [(t * 512, 512, 0) for t in range(8)] + [(L, 256, 1)]
            def pload(ti):
                c0, T, wch = tiles[ti]
                xt, bxt = xts[ti % 2]
                rt, brt = rts[ti % 2]
                K.dma(K.qs, lambda e: e.dma_start(out=xt[:, :, 0:T], in_=hTv[:, :, c0:c0 + T]), reads=[self.B["hT"]], writes=[bxt])
                K.dma(K.qs, lambda e: e.dma_start(out=rt[:, :, 0:T], in_=ropev[:, :, c0:c0 + T]), writes=[brt])

            def pnorm1(ti):
                c0, T, wch = tiles[ti]
                xt, bxt = xts[ti % 2]
                xn, bxn = xns[ti % 2]
                K.op(K.act, [lambda: nc.scalar.activation(out=xn[:, :, 0:T], in_=xt[:, :, 0:T], func=AF.Square)], reads=[bxt], writes=[bxn])
                fns = [(lambda k=k: nc.tensor.matmul(ps[7][:, 0:T], lhsT=self.ones[:, :], rhs=xn[:, k, 0:T], start=(k == 0), stop=(k == 7))) for k in range(8)]
                K.op(K.pe, fns, reads=[bxn, self.b_ones], writes=[psb[7]])
                K.op(K.act, [lambda: nc.scalar.activation(out=rs[:, 0:T], in_=ps[7][:, 0:T], func=AF.Sqrt, bias=self.epsb[:, 0:1], scale=1.0 / D)],
                     reads=[psb[7], self.b_eps], writes=[brs])
                K.op(K.dve, [lambda: nc.vector.reciprocal(out=rs[:, 0:T], in_=rs[:, 0:T])], reads=[brs], writes=[brs])

            def pnorm2(ti):
                c0, T, wch = tiles[ti]
                xt, bxt = xts[ti % 2]
                xn, bxn = xns[ti % 2]
                for c in range(8):
                    tm, btm = tmps[c % 2]
                    K.op(K.dve, [lambda c=c, tm=tm: nc.vector.scalar_tensor_tensor(
                        out=tm[:, 0:T], in0=xt[:, c, 0:T], scalar=self.ab(l, 1, wch, 0)[:, c:c + 1], in1=rs[:, 0:T], op0=ALU.mult, op1=ALU.mult)],
                        reads=[bxt, brs, self.b_AB], writes=[btm])
                    K.op(K.act, [lambda c=c, tm=tm: nc.scalar.activation(
                        out=xn[:, c, 0:T], in_=tm[:, 0:T], func=AF.Identity, bias=self.ab(l, 1, wch, 1)[:, c:c + 1], scale=1.0)],
                        reads=[btm, self.b_AB], writes=[bxn])

            pload(0)
            pnorm1(0)
            pnorm2(0)
            for ti, (c0, T, wch) in enumerate(tiles):
                ntc = T // 128
                xt, bxt = xts[ti % 2]
                rt, brt = rts[ti % 2]
                xn, bxn = xns[ti % 2]
                has_next = ti + 1 < len(tiles)
                if has_next:
                    pload(ti + 1)

                def fm_block(blk, pb):
                    fns = [(lambda k=k: nc.tensor.matmul(ps[pb][:, 0:T], lhsT=win[:, k, blk * 128:(blk + 1) * 128], rhs=xn[:, k, 0:T], start=(k == 0), stop=(k == 7)))
                           for k in range(8)]
                    K.op(K.pe, fns, reads=[bwin, bxn], writes=[psb[pb]])

                fm_block(2, 2)
                fm_block(3, 3)
                fm_block(4, 4)
                K.op(K.act, [lambda: nc.scalar.activation(out=sqa[:, 0:T], in_=ps[2][:, 0:T], func=AF.Square)], reads=[psb[2]], writes=[bsqa])
                K.op(K.act, [lambda: nc.scalar.activation(out=sqb[0:64, 0:T], in_=ps[3][0:64, 0:T], func=AF.Square)], reads=[psb[3]], writes=[bsqb])
                K.op(K.act, [lambda: nc.scalar.activation(out=sqc[:, 0:T], in_=ps[4][:, 0:T], func=AF.Square)], reads=[psb[4]], writes=[bsqc])
                fm_block(0, 0)
                fm_block(1, 1)
                K.op(K.act, [lambda: nc.scalar.copy(out=zF[:, 0, 0:T], in_=ps[0][:, 0:T])], reads=[psb[0]], writes=[bzF])
                K.op(K.act, [lambda: nc.scalar.copy(out=zF[:, 1, 0:T], in_=ps[1][:, 0:T])], reads=[psb[1]], writes=[bzF])
                K.op(K.pe, [lambda: nc.tensor.matmul(ps[5][:, 0:T], lhsT=self.ones[:, :], rhs=sqa[:, 0:T], start=True, stop=False),
                            lambda: nc.tensor.matmul(ps[5][:, 0:T], lhsT=self.ones[0:64, :], rhs=sqb[0:64, 0:T], start=False, stop=True)],
                     reads=[bsqa, bsqb, self.b_ones], writes=[psb[5]])
                K.op(K.pe, [lambda: nc.tensor.matmul(ps[6][:, 0:T], lhsT=self.ones[:, :], rhs=sqc[:, 0:T], start=True, stop=True)],
                     reads=[bsqc, self.b_ones], writes=[psb[6]])
                K.op(K.act, [lambda: nc.scalar.activation(out=rq[:, 0:T], in_=ps[5][:, 0:T], func=AF.Sqrt, bias=self.epsb[:, 0:1], scale=1.0 / 192)],
                     reads=[psb[5], self.b_eps], writes=[brq])
                K.op(K.act, [lambda: nc.scalar.activation(out=rkv[:, 0:T], in_=ps[6][:, 0:T], func=AF.Sqrt, bias=self.epsb[:, 0:1], scale=1.0 / 128)],
                     reads=[psb[6], self.b_eps], writes=[brkv])
                K.op(K.dve, [lambda: nc.vector.reciprocal(out=rq[:, 0:T], in_=rq[:, 0:T])], reads=[brq], writes=[brq])
                K.op(K.dve, [lambda: nc.vector.reciprocal(out=rkv[:, 0:T], in_=rkv[:, 0:T])], reads=[brkv], writes=[brkv])
                K.op(K.dve, [lambda: nc.vector.scalar_tensor_tensor(out=cqa[:, 0:T], in0=ps[2][:, 0:T], scalar=self.vcol(l, V_GQ0), in1=rq[:, 0:T], op0=ALU.mult, op1=ALU.mult)],
                     reads=[psb[2], brq, self.b_vecs], writes=[bcqa])
                K.op(K.dve, [lambda: nc.vector.scalar_tensor_tensor(out=cqb[0:64, 0:T], in0=ps[3][0:64, 0:T], scalar=self.vcol(l, V_GQ1)[0:64, :], in1=rq[0:64, 0:T], op0=ALU.mult, op1=ALU.mult)],
                     reads=[psb[3], brq, self.b_vecs], writes=[bcqb])
                K.op(K.dve, [lambda: nc.vector.scalar_tensor_tensor(out=ckvn[:, 0:T], in0=ps[4][:, 0:T], scalar=self.vcol(l, V_GKV), in1=rkv[:, 0:T], op0=ALU.mult, op1=ALU.mult)],
                     reads=[psb[4], brkv, self.b_vecs], writes=[bckvn])
                fm_block(5, 5)
                if has_next:
                    pnorm1(ti + 1)
                for tc in range(ntc):
                    pb = tc % 2
                    fns = [(lambda cq_=cq_, tc=tc, pb=pb: nc.tensor.matmul(ps[pb][:, :], lhsT=zF[:, cq_, tc * 128:(tc + 1) * 128], rhs=csw[:, cq_, wch, :], start=(cq_ == 0), stop=(cq_ == 1)))
                           for cq_ in range(2)]
                    K.op(K.pe, fns, reads=[bzF, bcsw], writes=[psb[pb]])
                    K.op(K.act, [lambda tc=tc, pb=pb: nc.scalar.copy(out=UVt[:, tc, :], in_=ps[pb][:, :])], reads=[psb[pb]], writes=[bUVt])
                K.dma(K.qs, lambda e, c0=c0, T=T, ntc=ntc: e.dma_start(out=S["UVd"][c0:c0 + T, :].rearrange("(k p) n -> p k n", p=128), in_=UVt[:, 0:ntc, :]),
                      reads=[bUVt], writes=[self.B["UVd"]])
                K.op(K.dve, [lambda: nc.vector.tensor_tensor(out=t1[64:96, 0:T], in0=ps[3][64:96, 0:T], in1=rt[64:96, 0, 0:T], op=ALU.mult)], reads=[psb[3], brt], writes=[bt1])
                K.op(K.dve, [lambda: nc.vector.tensor_tensor(out=t2[64:96, 0:T], in0=ps[5][64:96, 0:T], in1=rt[64:96, 1, 0:T], op=ALU.mult)], reads=[psb[5], brt], writes=[bt2])
                K.op(K.dve, [lambda: nc.vector.tensor_tensor(out=KTt[64:96, :, 0:T], in0=t1[64:96, 0:T].unsqueeze(1).broadcast_to([32, 4, T]),
                                                              in1=t2[64:96, 0:T].unsqueeze(1).broadcast_to([32, 4, T]), op=ALU.add)], reads=[bt1, bt2], writes=[bKTt])
                fm_block(6, 6)
                fm_block(7, 7)
                K.op(K.act, [lambda: nc.scalar.activation(out=uTt[:, 0, 0:T], in_=ps[6][:, 0:T], func=AF.Gelu_apprx_tanh)], reads=[psb[6]], writes=[buTt])
                K.op(K.act, [lambda: nc.scalar.activation(out=uTt[:, 1, 0:T], in_=ps[7][:, 0:T], func=AF.Gelu_apprx_tanh)], reads=[psb[7]], writes=[buTt])
                K.dma(K.qs, lambda e, c0=c0, T=T: e.dma_start(out=uTv[:, :, c0:c0 + T], in_=uTt[:, :, 0:T]), reads=[buTt], writes=[self.B["uTd"]])
                for tc in range(ntc):
                    pb = tc % 2
                    fns = [(lambda k=k, tc=tc, pb=pb: nc.tensor.matmul(ps[pb][:, :], lhsT=xn[:, k, tc * 128:(tc + 1) * 128], rhs=win[:, k, 1024:1536], start=(k == 0), stop=(k == 7)))
                           for k in range(8)]
                    K.op(K.pe, fns, reads=[bxn, bwin], writes=[psb[pb]])
                    K.op(K.act, [lambda pb=pb, tc=tc: nc.scalar.activation(out=gv[:, tc, :], in_=ps[pb][:, 0:256], func=AF.Gelu_apprx_tanh)], reads=[psb[pb]], writes=[bgv])
                    K.op(K.act, [lambda pb=pb, tc=tc: nc.scalar.copy(out=p_t[:, tc, :], in_=ps[pb][:, 256:512])], reads=[psb[pb]], writes=[bp_t])
                nh = ntc * 4
                K.op(K.dve, [lambda: nc.vector.tensor_tensor(out=sqv[:, 0:ntc, :], in0=gv[:, 0:ntc, :], in1=gv[:, 0:ntc, :], op=ALU.mult)], reads=[bgv], writes=[bsqv])
                K.op(K.dve, [lambda: nc.vector.tensor_reduce(out=ssv[:, 0:nh], in_=sqv[:, 0:ntc, :].rearrange("p t (h c) -> p (t h) c", c=64), axis=AX.X, op=ALU.add)],
                     reads=[bsqv], writes=[bssv])
                K.op(K.act, [lambda: nc.scalar.activation(out=ssv[:, 0:nh], in_=ssv[:, 0:nh], func=AF.Sqrt, bias=self.epsb[:, 0:1], scale=1.0 / 64)],
                     reads=[bssv, self.b_eps], writes=[bssv])
                K.op(K.dve, [lambda: nc.vector.reciprocal(out=ssv[:, 0:nh], in_=ssv[:, 0:nh])], reads=[bssv], writes=[bssv])
                K.op(K.dve, [lambda: nc.vector.tensor_tensor(out=sqv[:, 0:ntc, :], in0=gv[:, 0:ntc, :], in1=self.vcol(l, V_GSB, 256).unsqueeze(1).broadcast_to([128, ntc, 256]), op=ALU.mult)],
                     reads=[bgv, self.b_vecs, bsqv], writes=[bsqv])
                K.op(K.dve, [lambda: nc.vector.tensor_tensor(out=v_t[:, 0:ntc, :].rearrange("p t (h c) -> p (t h) c", c=64),
                                                              in0=sqv[:, 0:ntc, :].rearrange("p t (h c) -> p (t h) c", c=64),
                                                              in1=ssv[:, 0:nh].unsqueeze(2).broadcast_to([128, nh, 64]), op=ALU.mult)],
                     reads=[bsqv, bssv], writes=[bv_t])
                if has_next:
                    pnorm2(ti + 1)
                K.dma(K.qs, lambda e, c0=c0, T=T, ntc=ntc: e.dma_start(out=S["vd"][c0:c0 + T, :].rearrange("(k p) n -> p k n", p=128), in_=v_t[:, 0:ntc, :]), reads=[bv_t], writes=[self.B["vd"]])
                K.dma(K.qs, lambda e, c0=c0, T=T, ntc=ntc: e.dma_start(out=S["pd"][c0:c0 + T, :].rearrange("(k p) n -> p k n", p=128), in_=p_t[:, 0:ntc, :]), reads=[bp_t], writes=[self.B["pd"]])
                for h in range(4):
                    pa, pq = (2, 3) if h % 2 == 0 else (4, 5)
                    for (pb, w_) in ((pa, wuq), (pq, wuqp)):
                        K.op(K.pe, [lambda pb=pb, w_=w_, h=h: nc.tensor.matmul(ps[pb][0:96, 0:T], lhsT=w_[:, 0, h * 96:(h + 1) * 96], rhs=cqa[:, 0:T], start=True, stop=False),
                                    lambda pb=pb, w_=w_, h=h: nc.tensor.matmul(ps[pb][0:96, 0:T], lhsT=w_[0:64, 1, h * 96:(h + 1) * 96], rhs=cqb[0:64, 0:T], start=False, stop=True)],
                             reads=[bwuq, bwuqp, bcqa, bcqb], writes=[psb[pb]])
                    K.op(K.act, [lambda h=h, pa=pa: nc.scalar.copy(out=QTt[0:64, h, 0:T], in_=ps[pa][0:64, 0:T])], reads=[psb[pa]], writes=[bQTt])
                    K.op(K.dve, [lambda pa=pa: nc.vector.tensor_tensor(out=t1[64:96, 0:T], in0=ps[pa][64:96, 0:T], in1=rt[64:96, 0, 0:T], op=ALU.mult)], reads=[psb[pa], brt], writes=[bt1])
                    K.op(K.dve, [lambda pq=pq: nc.vector.tensor_tensor(out=t2[64:96, 0:T], in0=ps[pq][64:96, 0:T], in1=rt[64:96, 1, 0:T], op=ALU.mult)], reads=[psb[pq], brt], writes=[bt2])
                    K.op(K.dve, [lambda h=h: nc.vector.tensor_tensor(out=QTt[64:96, h, 0:T], in0=t1[64:96, 0:T], in1=t2[64:96, 0:T], op=ALU.add)], reads=[bt1, bt2], writes=[bQTt])
                K.dma(K.qs, lambda e, c0=c0, T=T: e.dma_start(out=QTv[:, :, c0:c0 + T], in_=QTt[0:96, :, 0:T]), reads=[bQTt], writes=[self.B["QTd"]])
                for h in range(4):
                    pb = 6 + h % 2
                    K.op(K.pe, [lambda pb=pb, h=h: nc.tensor.matmul(ps[pb][0:64, 0:T], lhsT=wukv[:, h * 128:h * 128 + 64], rhs=ckvn[:, 0:T], start=True, stop=True)],
                         reads=[bwukv, bckvn], writes=[psb[pb]])
                    K.op(K.act, [lambda pb=pb, h=h: nc.scalar.copy(out=KTt[0:64, h, 0:T], in_=ps[pb][0:64, 0:T])], reads=[psb[pb]], writes=[bKTt])
                K.dma(K.qs, lambda e, c0=c0, T=T: e.dma_start(out=KTv[:, :, c0:c0 + T], in_=KTt[0:96, :, 0:T]), reads=[bKTt], writes=[self.B["KTd"]])
                for tc in range(ntc):
                    pb = tc % 2
                    K.op(K.pe, [lambda pb=pb, tc=tc: nc.tensor.matmul(ps[pb][:, 0:256], lhsT=ckvn[:, tc * 128:(tc + 1) * 128], rhs=wv[:, :, :], start=True, stop=True)],
                         reads=[bwv, bckvn], writes=[psb[pb]])
                    K.op(K.dve, [lambda pb=pb, tc=tc: nc.vector.tensor_copy(out=Vt[:, tc, :], in_=ps[pb][:, 0:256])], reads=[psb[pb]], writes=[bVt])
                K.dma(K.qs, lambda e, c0=c0, T=T, ntc=ntc: e.dma_start(out=S["Vd"][c0:c0 + T, :].rearrange("(k p) n -> p k n", p=128), in_=Vt[:, 0:ntc, :]), reads=[bVt], writes=[self.B["Vd"]])
            K.barrier()

    def gen_dft(self, l, st, banks, use_act):
        nc, K, I, S = self.nc, self.K, self.I, self.S
        mixv = S["mixT"].rearrange("(c p) t -> p c t", p=128)
        ps, psb = self.ps, self.psb
        X, bX = self.sb(st, "fX", [128, 64 * 256], BF16)
        A, bA = self.sb(st, "fA", [128, 64 * 256], BF16)
        A3, bA3 = X, bX
        Ft, bFt = self.sb(st, "fFt", [128, 2, L], BF16)
        M3, bM3 = self.sb(st, "fM3", [128, 64 * 64], BF16)
        W1, bW1 = self.sb(st, "fW1", [128, 128], BF16)
        cases = [(64, 0, L, "fw1", "fm3")]
        if l == 0:
            cases.append((4, L, CL, "fw1c", "fm3c"))
        bank = 0
        ev = 0
        for (R, base, Lx, w1n, m3n) in cases:
            R2 = 2 * R
            K.dma(K.qs, lambda e, w1n=w1n, R2=R2: e.dma_start(out=W1[0:R2, 0:R2], in_=I[w1n][:, :]), writes=[bW1])
            K.dma(K.qs, lambda e, m3n=m3n, R=R: e.dma_start(out=M3[:, 0:R * 64], in_=I[m3n][:, :]), writes=[bM3])
            for uv in range(2):
                K.dma(K.qs, lambda e, uv=uv, R=R, base=base, Lx=Lx: e.dma_start(
                    out=X[uv * R:(uv + 1) * R, :].rearrange("p (b d) -> p b d", d=256),
                    in_=S["UVd"][base:base + Lx, uv * 256:(uv + 1) * 256].rearrange("(a b) d -> a b d", b=64)),
                    reads=[self.B["UVd"]], writes=[bX])
            yield ("d", 3)
            for cb in range(32):
                pb = banks[bank % len(banks)]
                bank += 1
                K.op(K.pe, [lambda cb=cb, pb=pb, R2=R2: nc.tensor.matmul(ps[pb][0:R2, :], lhsT=W1[0:R2, 0:R2], rhs=X[0:R2, cb * 512:(cb + 1) * 512], start=True, stop=True)],
                     reads=[bW1, bX], writes=[psb[pb]])
                ev += 1
                if use_act and ev % 2 == 0:
                    K.op(K.act, [lambda cb=cb, pb=pb, R2=R2: nc.scalar.copy(out=A[0:R2, cb * 512:(cb + 1) * 512], in_=ps[pb][0:R2, :])], reads=[psb[pb]], writes=[bA])
                else:
                    K.op(K.dve, [lambda cb=cb, pb=pb, R2=R2: nc.vector.tensor_copy(out=A[0:R2, cb * 512:(cb + 1) * 512], in_=ps[pb][0:R2, :])], reads=[psb[pb]], writes=[bA])
                if cb % 4 == 3:
                    yield None
            Adv = S["Ad"][0:2 * 64 * R * 256].rearrange("(r a b d) -> r a b d", r=2, a=64, b=R)
            for ri in range(2):
                K.dma(K.qs, lambda e, ri=ri, R=R, Adv=Adv: e.dma_start(
                    out=Adv[ri].rearrange("a b d -> b a d"), in_=A[ri * R:(ri + 1) * R, :].rearrange("p (a d) -> p a d", d=256)),
                    reads=[bA], writes=[self.B["Ad"]])
            K.dma(K.qs, lambda e, R=R, Adv=Adv: e.dma_start(out=A3[:, 0:R * 256], in_=S["Ad"][0:2 * 64 * R * 256].rearrange("(q f) -> q f", q=128)),
                  reads=[self.B["Ad"]], writes=[bA3])
            yield ("d", 4)
            A3v = A3[:, 0:R * 256].rearrange("p (b d) -> p b d", d=256)
            M3v = M3[:, 0:R * 64].rearrange("p (b c) -> p b c", c=64)
            ng = min(8, R)
            for dc in range(2):
                Ftv = Ft[:, dc, 0:Lx].rearrange("p (b a) -> p a b", a=R)
                for j in range(R // ng):
                    pb = banks[bank % len(banks)]
                    bank += 1
                    fns = [(lambda i=i, j=j, dc=dc, pb=pb: nc.tensor.matmul(ps[pb][:, i * 64:(i + 1) * 64], lhsT=A3v[:, j * ng + i, dc * 128:(dc + 1) * 128], rhs=M3v[:, j * ng + i, :], start=True, stop=True))
                           for i in range(ng)]
                    K.op(K.pe, fns, reads=[bA3, bM3], writes=[psb[pb]])
                    src = ps[pb][:, 0:ng * 64].rearrange("p (a b) -> p a b", b=64)
                    ev += 1
                    if use_act and ev % 2 == 0:
                        K.op(K.act, [lambda j=j, Ftv=Ftv, src=src: nc.scalar.copy(out=Ftv[:, j * ng:(j + 1) * ng, :], in_=src)], reads=[psb[pb]], writes=[bFt])
                    else:
                        K.op(K.dve, [lambda j=j, Ftv=Ftv, src=src: nc.vector.tensor_copy(out=Ftv[:, j * ng:(j + 1) * ng, :], in_=src)], reads=[psb[pb]], writes=[bFt])
                    if j % 2 == 1:
                        yield None
            K.dma(K.qs, lambda e, base=base, Lx=Lx: e.dma_start(out=mixv[:, 0:2, base:base + Lx], in_=Ft[:, :, 0:Lx]), reads=[bFt], writes=[self.B["mixT"]])
            yield None

    def stage_attdft(self, l):
        with ExitStack() as st:
            ga = self.gen_att(l, st)
            gd = self.gen_dft(l, st, banks=(6, 7), use_act=False)
            hold = 0
            d_alive = True
            for _ in ga:
                if not d_alive:
                    continue
                if hold > 0:
                    hold -= 1
                    continue
                for _k in range(3):
                    try:
                        r = next(gd)
                    except StopIteration:
                        d_alive = False
                        break
                    if r is not None:
                        hold = r[1]
                        break
            if d_alive:
                for _ in gd:
                    pass
            self.K.barrier()

    def gen_att(self, l, st):
        nc, K, I, S = self.nc, self.K, self.I, self.S
        mixv = S["mixT"].rearrange("(c p) t -> p c t", p=128)
        QTv = S["QTd"].rearrange("r (h t) -> r h t", h=4)
        KTv = S["KTd"].rearrange("r (h t) -> r h t", h=4)
        ps, psb = self.ps, self.psb
        if True:
            KT, bKT = self.sb(st, "KT", [128, 4, NT], BF16)
            Vr, bVr = self.sb(st, "Vr", [128, 34, 512], BF16)
            Qts = [self.sb(st, f"Qt{i}", [128, 4, 512], BF16) for i in range(2)]
            Pts = [self.sb(st, f"Pt{i}", [128, 1024], BF16) for i in range(3)]
            rec, brec = self.sb(st, "rec", [128, 512], F32)
            atts = [self.sb(st, f"att{i}", [128, 2, 512], BF16) for i in range(2)]
            K.dma(K.qs, lambda e: e.dma_start(out=KT[0:96, :, :], in_=KTv[:, :, :]), reads=[self.B["KTd"]], writes=[bKT])
            Vr4 = Vr[:, :, :].rearrange("p k (h c) -> p k h c", c=128)
            K.op(K.dve, [lambda: nc.vector.memset(Vr4[:, :, :, 64:128], 1.0)], writes=[bVr])
            Vdv = S["Vd"].rearrange("(k p) n -> p k n", p=128)
            for hh_ in range(4):
                K.dma(K.qs, lambda e, hh_=hh_: e.dma_start(out=Vr4[:, :, hh_, 0:64], in_=Vdv[:, :, hh_ * 64:(hh_ + 1) * 64]),
                      reads=[self.B["Vd"]], writes=[bVr])
            qtiles = [(n * 512, 512, list(range(34))) for n in range(8)]
            if l == 0:
                qtiles.append((L, 256, [32, 33]))
            pi = 0
            sbi = 0
            for qi, (c0, W, chunks) in enumerate(qtiles):
                Qt, bQt = Qts[qi % 2]
                at, bat = atts[qi % 2]
                K.dma(K.qs, lambda e, Qt=Qt, c0=c0, W=W: e.dma_start(out=Qt[0:96, :, 0:W], in_=QTv[:, :, c0:c0 + W]), reads=[self.B["QTd"]], writes=[bQt])
                for h in range(4):
                    ob = 4 + (h % 2)
                    vs = (h * 128, h * 128 + 128) if h % 2 == 0 else (h * 128 - 64, h * 128 + 64)
                    npair = len(chunks) // 2
                    pend = []
                    for i in range(npair + 1):
                        if i < npair:
                            k0, k1 = chunks[2 * i], chunks[2 * i + 1]
                            b0 = 2 * (sbi % 2)
                            sbi += 1
                            Pt, bPt = Pts[pi % 3]
                            pi += 1
                            K.op(K.pe, [lambda kc=k0, b0=b0, h=h, Qt=Qt, W=W: nc.tensor.matmul(ps[b0][:, 0:W], lhsT=KT[0:96, h, kc * 128:(kc + 1) * 128], rhs=Qt[0:96, h, 0:W], start=True, stop=True),
                                        lambda kc=k1, b0=b0, h=h, Qt=Qt, W=W: nc.tensor.matmul(ps[b0 + 1][:, 0:W], lhsT=KT[0:96, h, kc * 128:(kc + 1) * 128], rhs=Qt[0:96, h, 0:W], start=True, stop=True)],
                                 reads=[bKT, bQt], writes=[psb[b0], psb[b0 + 1]])
                            src = self.psall[:, b0 * 512:(b0 + 2) * 512].rearrange("p (k c) -> p k c", c=512)[:, :, 0:W]
                            dst = Pt[:, :].rearrange("p (k c) -> p k c", c=512)[:, :, 0:W]
                            K.op(K.act, [lambda src=src, dst=dst: nc.scalar.activation(out=dst, in_=src, func=AF.Exp, scale=SCALE)],
                                 reads=[psb[b0], psb[b0 + 1]], writes=[bPt])
                            pend.append((k0, k1, Pt, bPt))
                        if i >= 1:
                            j = i - 1
                            k0, k1, Pt, bPt = pend[j]
                            K.op(K.pe, [lambda kc=k0, Pt=Pt, ob=ob, vs=vs, W=W, j=j: nc.tensor.matmul(ps[ob][:, 0:W], lhsT=Vr[:, kc, vs[0]:vs[1]], rhs=Pt[:, 0:W], start=(j == 0), stop=False),
                                        lambda kc=k1, Pt=Pt, ob=ob, vs=vs, W=W, j=j, npair=npair: nc.tensor.matmul(ps[ob][:, 0:W], lhsT=Vr[:, kc, vs[0]:vs[1]], rhs=Pt[:, 512:512 + W], start=False, stop=(j == npair - 1))],
                                 reads=[bVr, bPt], writes=[psb[ob]])
                    yield None
                    if h % 2 == 0:
                        K.op(K.dve, [lambda ob=ob, W=W: nc.vector.reciprocal(out=rec[64:128, 0:W], in_=ps[ob][64:128, 0:W])], reads=[psb[ob]], writes=[brec])
                        K.op(K.dve, [lambda ob=ob, W=W, h=h, at=at: nc.vector.tensor_tensor(out=at[0:64, h // 2, 0:W], in0=ps[ob][0:64, 0:W], in1=rec[64:128, 0:W], op=ALU.mult)],
                             reads=[psb[ob], brec], writes=[bat])
                    else:
                        K.op(K.dve, [lambda ob=ob, W=W: nc.vector.reciprocal(out=rec[0:64, 0:W], in_=ps[ob][0:64, 0:W])], reads=[psb[ob]], writes=[brec])
                        K.op(K.dve, [lambda ob=ob, W=W, h=h, at=at: nc.vector.tensor_tensor(out=at[64:128, h // 2, 0:W], in0=ps[ob][64:128, 0:W], in1=rec[0:64, 0:W], op=ALU.mult)],
                             reads=[psb[ob], brec], writes=[bat])
                K.dma(K.qs, lambda e, at=at, c0=c0, W=W: e.dma_start(out=mixv[:, 2:4, c0:c0 + W], in_=at[:, :, 0:W]), reads=[bat], writes=[self.B["mixT"]])
            yield None

    def stage_sgp(self, l):
        nc, K, I, S = self.nc, self.K, self.I, self.S
        mixv = S["mixT"].rearrange("(c p) t -> p c t", p=128)
        uTv = S["uTd"].rearrange("(c p) t -> p c t", p=128)
        ps, psb = self.ps, self.psb
        with ExitStack() as st:
            wsT, bwsT = self.sb(st, "wsT", [128, 4, 128], BF16)
            bands, bbands = self.sb(st, "bands", [128, 20, 128], BF16)
            wpl, bwpl = self.sb(st, "wpl", [128, 2, 64], BF16)
            vr, bvr = self.sb(st, "vr", [128, 34, 256], BF16)
            pr, bpr = self.sb(st, "pr", [128, 34, 256], BF16)
            uts = [self.sb(st, f"ut{i}", [128, 2, 512], BF16) for i in range(2)]
            tmps_ = [self.sb(st, f"stmp{i}", [128, 2, 512], F32) for i in range(2)]
            pooleds = [self.sb(st, f"pooled{i}", [128, 2, 512], BF16) for i in range(2)]
            mts = [self.sb(st, f"mt{i}", [128, 4, 512], BF16) for i in range(2)]
            K.dma(K.qg, lambda e: e.dma_start(out=wsT[:, :, :], in_=I["wsT"][l].rearrange("q (h p) -> q h p", h=4)), writes=[bwsT])
            K.dma(K.qg, lambda e: e.dma_start(out=wpl[:, :, :], in_=I["wpl"][l].rearrange("r (g d) -> r g d", g=2)), writes=[bwpl])
            K.dma(K.qs, lambda e: e.dma_start(out=bands[:, :, :], in_=I["bands"].rearrange("p (v t) -> p v t", t=128)), writes=[bbands])
            K.dma(K.qs, lambda e: e.dma_start(out=vr[:, :, :], in_=S["vd"].rearrange("(k p) n -> p k n", p=128)), reads=[self.B["vd"]], writes=[bvr])
            K.dma(K.qs, lambda e: e.dma_start(out=pr[:, :, :], in_=S["pd"].rearrange("(k p) n -> p k n", p=128)), reads=[self.B["pd"]], writes=[bpr])
            tiles = [(t * 512, 512, 0, 31) for t in range(8)]
            if l == 0:
                tiles.append((L, 256, 32, 33))
            def sg_a(ti):
                c0, T, cfirst, clast = tiles[ti]
                nck = T // 128
                ut, but = uts[ti % 2]
                bo = 4 * (ti % 2)
                K.dma(K.qs, lambda e: e.dma_start(out=ut[:, :, 0:T], in_=uTv[:, :, c0:c0 + T]), reads=[self.B["uTd"]], writes=[but])
                for hp in range(2):
                    pb = bo + hp
                    fns = []
                    for ck in range(nck):
                        n = c0 // 128 + ck
                        for hh in range(2):
                            h = 2 * hp + hh
                            fns.append(lambda ck=ck, n=n, hh=hh, h=h, pb=pb: nc.tensor.matmul(ps[pb][hh * 64:(hh + 1) * 64, ck * 128:(ck + 1) * 128], lhsT=vr[:, n, h * 64:(h + 1) * 64], rhs=wsT[:, h, :], start=True, stop=True))
                    K.op(K.pe, fns, reads=[bvr, bwsT], writes=[psb[pb]])
                for gp in range(2):
                    pb = bo + 2 + gp
                    fns = []
                    for ck in range(nck):
                        n = c0 // 128 + ck
                        if n == cfirst:
                            srcs = [(n, 3), (n + 1, 2)]
                        elif n == clast:
                            srcs = [(n - 1, 0), (n, 4)]
                        else:
                            srcs = [(n - 1, 0), (n, 1), (n + 1, 2)]
                        for gg in range(2):
                            g_ = 2 * gp + gg
                            for si, (src, var) in enumerate(srcs):
                                fns.append(lambda ck=ck, gg=gg, g_=g_, src=src, var=var, si=si, ns=len(srcs), pb=pb: nc.tensor.matmul(
                                    ps[pb][gg * 64:(gg + 1) * 64, ck * 128:(ck + 1) * 128], lhsT=pr[:, src, g_ * 64:(g_ + 1) * 64], rhs=bands[:, g_ * 5 + var, :],
                                    start=(si == 0), stop=(si == ns - 1)))
                    K.op(K.pe, fns, reads=[bpr, bbands], writes=[psb[pb]])

            def sg_b(ti):
                c0, T, cfirst, clast = tiles[ti]
                nck = T // 128
                ut, but = uts[ti % 2]
                mt, bmt = mts[ti % 2]
                tmp, btmp = tmps_[ti % 2]
                pooled, bpooled = pooleds[ti % 2]
                bo = 4 * (ti % 2)
                for gp in range(2):
                    pb = bo + 2 + gp
                    K.op(K.act, [lambda gp=gp, pb=pb: nc.scalar.copy(out=pooled[:, gp, 0:T], in_=ps[pb][:, 0:T])], reads=[psb[pb]], writes=[bpooled])
                for hp in range(2):
                    pb = bo + hp
                    K.op(K.dve, [lambda pb=pb, hp=hp: nc.vector.tensor_tensor(out=tmp[:, hp, 0:T].rearrange("p (k c) -> p k c", c=128), in0=ps[pb][:, 0:T].rearrange("p (k c) -> p k c", c=128),
                                                                          in1=self.vcol(l, V_BSB + hp * 128, 128).unsqueeze(1).broadcast_to([128, nck, 128]), op=ALU.add)],
                         reads=[psb[pb], self.b_vecs], writes=[btmp])
                    K.op(K.dve, [lambda hp=hp: nc.vector.tensor_tensor(out=mt[:, hp, 0:T], in0=tmp[:, hp, 0:T], in1=ut[:, hp, 0:T], op=ALU.mult)],
                         reads=[btmp, but], writes=[bmt])
                for gp in range(2):
                    pb2 = bo + gp
                    fns = [(lambda gg=gg, gp=gp, pb2=pb2: nc.tensor.matmul(ps[pb2][gg * 64:(gg + 1) * 64, 0:T], lhsT=wpl[gg * 64:(gg + 1) * 64, gp, :], rhs=pooled[gg * 64:(gg + 1) * 64, gp, 0:T], start=True, stop=True))
                           for gg in range(2)]
                    K.op(K.pe, fns, reads=[bwpl, bpooled], writes=[psb[pb2]])
                    K.op(K.dve, [lambda gp=gp, pb2=pb2: nc.vector.tensor_scalar(out=mt[:, 2 + gp, 0:T], in0=ps[pb2][:, 0:T], scalar1=self.vcol(l, V_SP + gp), scalar2=None, op0=ALU.mult)],
                         reads=[psb[pb2], self.b_vecs], writes=[bmt])
                K.dma(K.qs, lambda e: e.dma_start(out=mixv[:, 4:8, c0:c0 + T], in_=mt[:, :, 0:T]), reads=[bmt], writes=[self.B["mixT"]])

            sg_a(0)
            for ti in range(len(tiles)):
                if ti + 1 < len(tiles):
                    sg_a(ti + 1)
                sg_b(ti)
            K.barrier()

    def stage_final(self):
        nc, K, S = self.nc, self.K, self.S
        hTv = S["hT"].rearrange("(c p) t -> p c t", p=128)
        with ExitStack() as st:
            self.ensure_eps(st)
            xts = [self.sb(st, f"fx{i}", [128, 8, 512], F32) for i in range(2)]
            sqs = [self.sb(st, f"fsq{i}", [128, 8, 512], BF16) for i in range(2)]
            rss = [self.sb(st, f"frs{i}", [128, 512], F32) for i in range(2)]
            ys = [self.sb(st, f"fy{i}", [128, 4, D], F32) for i in range(2)]
            gf = self.vecs[:, 2 * NVL:2 * NVL + 8]
            def fin_n(t):
                xt, bxt = xts[t % 2]
                yo, byo = ys[t % 2]
                sq, bsq = sqs[t % 2]
                rs, brs = rss[t % 2]
                K.dma(K.qs, lambda e, t=t, xt=xt: e.dma_start(out=xt[:, :, :], in_=hTv[:, :, t * 512:(t + 1) * 512]), reads=[self.B["hT"]], writes=[bxt])
                K.op(K.act, [lambda xt=xt, sq=sq: nc.scalar.activation(out=sq[:, :, :], in_=xt[:, :, :], func=AF.Square)], reads=[bxt], writes=[bsq])
                fns = [(lambda k=k, sq=sq: nc.tensor.matmul(self.ps[0][:, :], lhsT=self.ones[:, :], rhs=sq[:, k, :], start=(k == 0), stop=(k == 7))) for k in range(8)]
                K.op(K.pe, fns, reads=[bsq, self.b_ones], writes=[self.psb[0]])
                K.op(K.act, [lambda rs=rs: nc.scalar.activation(out=rs[:, :], in_=self.ps[0][:, :], func=AF.Sqrt, bias=self.epsb[:, 0:1], scale=1.0 / D)],
                     reads=[self.psb[0], self.b_eps], writes=[brs])
                K.op(K.dve, [lambda rs=rs: nc.vector.reciprocal(out=rs[:, :], in_=rs[:, :])], reads=[brs], writes=[brs])
                for c in range(8):
                    K.op(K.dve, [lambda c=c, xt=xt, rs=rs: nc.vector.scalar_tensor_tensor(
                        out=xt[:, c, :], in0=xt[:, c, :], scalar=gf[:, c:c + 1], in1=rs[:, :], op0=ALU.mult, op1=ALU.mult)],
                        reads=[bxt, brs, self.b_vecs], writes=[bxt])

            def fin_t(t):
                xt, bxt = xts[t % 2]
                yo, byo = ys[t % 2]
                for c in range(8):
                    pb = 1 + (c % 7)
                    fns = [(lambda k=k, c=c, pb=pb, xt=xt: nc.tensor.transpose(self.ps[pb][:, k * 128:(k + 1) * 128], xt[:, c, k * 128:(k + 1) * 128], self.ident[:, :]))
                           for k in range(4)]
                    K.op(K.pe, fns, reads=[bxt, self.b_ident], writes=[self.psb[pb]])
                    psv = self.ps[pb][:, :].rearrange("p (k d) -> p k d", d=128)
                    if c % 2 == 0:
                        K.op(K.act, [lambda c=c, psv=psv, yo=yo: nc.scalar.copy(out=yo[:, :, c * 128:(c + 1) * 128], in_=psv)], reads=[self.psb[pb]], writes=[byo])
                    else:
                        K.op(K.dve, [lambda c=c, psv=psv, yo=yo: nc.vector.tensor_copy(out=yo[:, :, c * 128:(c + 1) * 128], in_=psv)], reads=[self.psb[pb]], writes=[byo])
                outv = self.out[t * 512:(t + 1) * 512, :].rearrange("(k p) d -> p k d", p=128)
                K.dma(K.qs, lambda e, yo=yo, outv=outv: e.dma_start(out=outv, in_=yo[:, :, :]), reads=[byo], writes=[self.B["out"]])

            fin_n(0)
            for t in range(8):
                if t + 1 < 8:
                    fin_n(t + 1)
                fin_t(t)
            K.barrier()


def _rope_perm():
    p = np.zeros(32, np.int64)
    for a in range(2):
        for half in range(2):
            for j in range(8):
                p[a * 16 + half * 8 + j] = a * 16 + (1 - half) * 8 + j
    return p


def _constants():
    bf = ml_dtypes.bfloat16
    C = {}
    C["ident"] = np.eye(128, dtype=np.float32)
    cc = np.arange(64, dtype=np.float64)
    ang = 2 * np.pi * np.outer(cc, cc) / 64.0
    c64 = np.zeros((64, 4, 2, 128), np.float32)
    for kind, (f, Lx) in enumerate(((np.cos, L), (np.sin, L), (np.cos, CL), (np.sin, CL))):
        m = f(ang) / math.sqrt(64.0 * Lx)
        c64[:, kind, 0, 0:64] = m.T
        c64[:, kind, 1, 64:128] = m.T
    C["c64"] = c64.reshape(64, -1)
    t = np.arange(L)
    row = (t // 64).astype(np.float32)
    col = (t % 64).astype(np.float32)
    inv = np.power(np.float32(10000.0), -np.arange(0, 16, 2, dtype=np.float32) / np.float32(16)).astype(np.float32)
    rope = np.zeros((128, 2, NT), np.float32)
    rope[64:96, 0, L:] = 1.0
    for a, pos in enumerate((row, col)):
        angp = (pos[None, :] * inv[:, None]).astype(np.float32)
        cs, sn = np.cos(angp).astype(np.float32), np.sin(angp).astype(np.float32)
        for half in range(2):
            r0 = 64 + a * 16 + half * 8
            rope[r0:r0 + 8, 0, :L] = cs
            rope[r0:r0 + 8, 1, :L] = -sn if half == 0 else sn
    C["rope"] = rope.reshape(128, -1)
    bands = np.zeros((128, 4, 5, 128), np.float32)
    for gi, wdw in enumerate((2, 4, 8, 16)):
        for var in range(5):
            for tp in range(128):
                lo_rel, hi_rel = tp - wdw // 2, tp - wdw // 2 + wdw
                if var == 3:
                    lo_c, hi_c = max(lo_rel, 0), hi_rel
                elif var == 4:
                    lo_c, hi_c = lo_rel, min(hi_rel, 128)
                else:
                    lo_c, hi_c = lo_rel, hi_rel
                cnt = float(hi_c - lo_c)
                src_off = {0: -128, 1: 0, 2: 128, 3: 0, 4: 0}[var]
                for ts in range(128):
                    pos = ts + src_off
                    v = 0.0
                    if lo_c <= pos < hi_c:
                        v += 1.0 / cnt
                    if pos == tp:
                        v -= 1.0
                    bands[ts, gi, var, tp] = v
    C["bands"] = bands.reshape(128, -1).astype(bf)
    for R, nm1, nm3 in ((64, "fw1", "fm3"), (4, "fw1c", "fm3c")):
        Lx = 64 * R
        a1 = np.arange(R, dtype=np.float64)
        ang1 = 2 * np.pi * np.outer(a1, a1) / R
        Cw, Sw = np.cos(ang1), np.sin(ang1)
        W1 = np.zeros((2 * R, 2 * R), np.float64)
        W1[0:R, 0:R] = Cw
        W1[R:, 0:R] = -Sw
        W1[0:R, R:] = -Sw
        W1[R:, R:] = -Cw
        C[nm1] = W1.astype(np.float32).astype(bf)
        l2 = np.arange(64, dtype=np.float64)[:, None, None]
        l1p = np.arange(R, dtype=np.float64)[None, :, None]
        l2p = np.arange(64, dtype=np.float64)[None, None, :]
        ang3 = 2 * np.pi * l2 * (l1p + R * l2p) / Lx
        M3 = np.concatenate([np.cos(ang3), np.sin(ang3)], axis=0)
        C[nm3] = M3.reshape(128, R * 64).astype(np.float32).astype(bf)
    return C


_CONST = None


def _prep_inputs(inp):
    global _CONST
    if _CONST is None:
        _CONST = _constants()
    f = lambda a: np.ascontiguousarray(np.asarray(a, dtype=np.float32))
    sh = {}
    sh["wada"] = f(inp["w_ada"])
    sh["w13a"] = f(inp["w13_ffn1"]); sh["w2a"] = f(inp["w2_ffn1"])
    sh["w13b"] = f(inp["w13_ffn2"]); sh["w2b"] = f(inp["w2_ffn2"])
    w_in = f(inp["w_in"])
    perm = _rope_perm()
    winx = np.zeros((2, D, 1536), np.float32)
    winx[:, :, 0:256] = w_in[:, :, 0:256]
    winx[:, :, 256:384] = w_in[:, :, 256:384]
    winx[:, :, 384:448] = w_in[:, :, 384:448]
    winx[:, :, 448:480] = w_in[:, :, 576:608]
    winx[:, :, 512:640] = w_in[:, :, 448:576]
    winx[:, :, 640 + 64:640 + 96] = w_in[:, :, 576:608][:, :, perm]
    winx[:, :, 768:1024] = w_in[:, :, 608:864]
    winx[:, :, 1024:1536] = w_in[:, :, 864:1376]
    sh["winx"] = winx
    w_uq = f(inp["w_uq"])
    wuqp = w_uq.copy()
    for h in range(4):
        wuqp[:, :, h * 96 + 64:h * 96 + 96] = w_uq[:, :, h * 96 + 64:h * 96 + 96][:, :, perm]
    sh["wuq"] = w_uq; sh["wuqp"] = wuqp
    sh["wukv"] = f(inp["w_ukv"])
    sh["wfn"] = np.ascontiguousarray(f(inp["w_fnet"]).transpose(0, 2, 1, 3).reshape(2, 64, 256))
    sh["wsT"] = np.ascontiguousarray(f(inp["w_sgu"]).transpose(0, 3, 1, 2).reshape(2, 128, 512))
    wp = f(inp["w_pool"])
    wpl = np.zeros((2, 128, 2, 64), np.float32)
    for g in range(4):
        wpl[:, (g % 2) * 64:(g % 2) * 64 + 64, g // 2, :] = wp[:, g]
    sh["wpl"] = wpl.reshape(2, 128, 128)
    sh["wout"] = f(inp["w_out"])
    vecs = np.zeros((128, 2 * NVL + 8), np.float32)
    fm = lambda v: np.asarray(v, np.float32).reshape(-1, 128).T
    for l in range(2):
        o = l * NVL
        vecs[:, o + V_GF1:o + V_GF1 + 8] = fm(inp["g_ffn1"][l])
        vecs[:, o + V_GMIX:o + V_GMIX + 8] = fm(inp["g_mix"][l])
        vecs[:, o + V_GF2:o + V_GF2 + 8] = fm(inp["g_ffn2"][l])
        vecs[:, o + V_BADA:o + V_BADA + 72] = fm(inp["b_ada"][l])
        gq = np.asarray(inp["g_q"][l], np.float32)
        vecs[:, o + V_GQ0] = gq[0:128]
        vecs[0:64, o + V_GQ1] = gq[128:192]
        vecs[:, o + V_GKV] = np.asarray(inp["g_kv"][l], np.float32)
        vecs[:, o + V_SP:o + V_SP + 2] = fm(inp["s_pool"][l])
        bs = np.asarray(inp["b_sgu"][l], np.float32)
        bsb = np.zeros((128, 2, 128), np.float32)
        for h in range(4):
            bsb[(h % 2) * 64:(h % 2) * 64 + 64, h // 2, :] = bs[h][None, :]
        vecs[:, o + V_BSB:o + V_BSB + 256] = bsb.reshape(128, 256)
        vecs[:, o + V_GSB:o + V_GSB + 256] = np.asarray(inp["g_sgu"][l], np.float32).reshape(1, 256)
    vecs[:, 2 * NVL:2 * NVL + 8] = fm(inp["g_final"])
    sh["vecs"] = vecs
    sh.update(_CONST)
    x = f(inp["x"]); ctx = f(inp["ctx"]); c = f(inp["c"]); c_ctx = f(inp["c_ctx"])
    maps = []
    for b in range(x.shape[0]):
        m = dict(sh)
        m["x"] = x[b]
        m["ctx"] = ctx[b]
        cc = np.zeros((128, 8, 2), np.float32)
        cc[:, :, 0] = c[b].reshape(8, 128).T
        cc[:, :, 1] = c_ctx.reshape(8, 128).T
        m["cc"] = cc.reshape(128, 16)
        maps.append(m)
    return maps


_NC_CACHE = {}


def kernel(**inputs):
    maps = _prep_inputs(inputs)
    if "full" not in _NC_CACHE:
        _NC_CACHE["full"] = Prog({}).build()
    nc = _NC_CACHE["full"]
    res = run_bass_kernel_spmd(nc, maps, core_ids=list(range(len(maps))))
    return np.stack([np.asarray(r["out"], dtype=np.float32) for r in res.results], axis=0)
```

```python
import math
from contextlib import ExitStack

import numpy as np
import ml_dtypes

import concourse.bass as bass
import concourse.mybir as mybir
from concourse.bass_utils import run_bass_kernel_spmd

F32 = mybir.dt.float32
BF16 = mybir.dt.bfloat16
AF = mybir.ActivationFunctionType
ALU = mybir.AluOpType
AX = mybir.AxisListType

D = 1024
L = 4096
CL = 256
NT = L + CL
DFF = 2816
NJ = DFF // 128
EPS = 1e-6
SCALE = 96 ** -0.5
NVL = 613
V_GF1, V_GMIX, V_GF2, V_BADA, V_GQ0, V_GQ1, V_GKV, V_SP, V_BSB, V_GSB = 0, 8, 16, 24, 96, 97, 98, 99, 101, 357


class Ev:
    __slots__ = ("sem", "key", "val", "know")

    def __init__(self, sem, key, val, know):
        self.sem, self.key, self.val, self.know = sem, key, val, know


class Buf:
    __slots__ = ("w", "r", "multi", "name")

    def __init__(self, name="", multi=False):
        self.w, self.r, self.multi, self.name = {}, {}, multi, name


class Eng:
    def __init__(self, name, h, sem, inorder=False):
        self.name, self.h, self.sem, self.key = name, h, sem, "E_" + name
        self.cnt = 0
        self.know = {}
        self.inorder = inorder


class Queue:
    def __init__(self, eng, sems, name):
        self.eng, self.sems, self.name = eng, sems, name
        self.n = 0


class Ctx:
    def __init__(self, nc, st, nslots=12):
        self.nc = nc
        mk = lambda n: st.enter_context(nc.semaphore(n))
        self.pe = Eng("pe", nc.tensor, mk("s_pe"), inorder=True)
        self.act = Eng("act", nc.scalar, mk("s_act"))
        self.dve = Eng("dve", nc.vector, mk("s_dve"))
        self.pool = Eng("pool", nc.gpsimd, mk("s_pool"))
        self.sp = Eng("sp", nc.sync, mk("s_sp"))
        self.engs = [self.pe, self.act, self.dve, self.pool, self.sp]
        self.qs = Queue(self.sp, [mk(f"q_s{i}") for i in range(nslots)], "qs")
        self.qg = Queue(self.pool, [mk(f"q_g{i}") for i in range(nslots)], "qg")
        self.queues = [self.qs, self.qg]
        self.n_wait = 0

    def _need(self, eng, evs):
        for e in evs:
            if eng.know.get(e.key, 0) >= e.val:
                continue
            if e.key == eng.key and eng.inorder:
                continue
            eng.h.wait_ge(e.sem, e.val)
            self.n_wait += 1
            kn = eng.know
            for k, v in e.know.items():
                if kn.get(k, 0) < v:
                    kn[k] = v
            kn[e.key] = e.val

    @staticmethod
    def _deps(reads, writes):
        evs = []
        for b in reads:
            evs.extend(b.w.values())
        for b in writes:
            if not b.multi:
                evs.extend(b.w.values())
            evs.extend(b.r.values())
        return evs

    @staticmethod
    def _update(ev, reads, writes):
        for b in writes:
            if b.multi:
                b.w[ev.key] = ev
            else:
                b.w = {ev.key: ev}
                b.r = {}
        for b in reads:
            b.r[ev.key] = ev

    def op(self, eng, fns, reads=(), writes=()):
        self._need(eng, self._deps(reads, writes))
        ins = None
        for fn in fns:
            ins = fn()
        eng.cnt += 1
        ins.then_inc(eng.sem, 1)
        know = dict(eng.know)
        know[eng.key] = eng.cnt
        ev = Ev(eng.sem, eng.key, eng.cnt, know)
        self._update(ev, reads, writes)
        return ev

    def dma(self, q, fn, reads=(), writes=()):
        eng = q.eng
        self._need(eng, self._deps(reads, writes))
        ns = len(q.sems)
        slot, rnd = q.n % ns, q.n // ns
        key = f"{q.name}{slot}"
        sem = q.sems[slot]
        if rnd > 0 and eng.know.get(key, 0) < 16 * rnd:
            eng.h.wait_ge(sem, 16 * rnd)
            self.n_wait += 1
            eng.know[key] = 16 * rnd
        ins = fn(eng.h)
        ins.then_inc(sem, 16)
        q.n += 1
        ev = Ev(sem, key, 16 * (rnd + 1), dict(eng.know))
        self._update(ev, reads, writes)
        return ev

    def all_events(self):
        evs = []
        for e in self.engs:
            if e.cnt > 0:
                evs.append(Ev(e.sem, e.key, e.cnt, {}))
        for q in self.queues:
            ns = len(q.sems)
            for s in range(min(ns, q.n)):
                last = ((q.n - 1 - s) // ns) * ns + s
                evs.append(Ev(q.sems[s], f"{q.name}{s}", 16 * (last // ns + 1), {}))
        return evs

    def barrier(self, engs=None):
        evs = self.all_events()
        for e in (engs or self.engs):
            self._need(e, evs)


class Prog:
    def __init__(self, cfg):
        self.cfg = cfg
        self.nc = bass.Bass("TRN2", target_bir_lowering=False)
        self.uid = 0

    def din(self, name, shape, dt=F32):
        return self.nc.dram_tensor(name, list(shape), dt, kind="ExternalInput").ap()

    def dscr(self, name, shape, dt):
        kind = "ExternalOutput" if name in self.cfg.get("debug", ()) else "Internal"
        return self.nc.dram_tensor(name, list(shape), dt, kind=kind).ap()

    def sb(self, st, name, shape, dt, multi=False):
        self.uid += 1
        t = st.enter_context(self.nc.sbuf_tensor(f"{name}_{self.uid}", list(shape), dt))
        return t, Buf(name, multi)

    def build(self):
        nc, cfg = self.nc, self.cfg
        nl = cfg.get("n_layers", 2)
        I = {}
        I["x"] = self.din("x", [L, D])
        I["ctx"] = self.din("ctx", [CL, D])
        I["cc"] = self.din("cc", [128, 16])
        I["wada"] = self.din("wada", [2, D, 9 * D])
        I["w13a"] = self.din("w13a", [2, D, 2 * DFF])
        I["w2a"] = self.din("w2a", [2, DFF, D])
        I["w13b"] = self.din("w13b", [2, D, 2 * DFF])
        I["w2b"] = self.din("w2b", [2, DFF, D])
        I["winx"] = self.din("winx", [2, D, 1536])
        I["wuq"] = self.din("wuq", [2, 192, 384])
        I["wuqp"] = self.din("wuqp", [2, 192, 384])
        I["wukv"] = self.din("wukv", [2, 128, 512])
        I["wfn"] = self.din("wfn", [2, 64, 4 * 64])
        I["wsT"] = self.din("wsT", [2, 128, 4 * 128])
        I["wpl"] = self.din("wpl", [2, 128, 2 * 64])
        I["wout"] = self.din("wout", [2, D, D])
        I["vecs"] = self.din("vecs", [128, 2 * NVL + 8])
        I["ident"] = self.din("ident", [128, 128])
        I["c64"] = self.din("c64", [64, 4 * 2 * 128])
        I["rope"] = self.din("rope", [128, 2 * NT])
        I["bands"] = self.din("bands", [128, 4 * 5 * 128], BF16)
        I["fw1"] = self.din("fw1", [128, 128], BF16)
        I["fw1c"] = self.din("fw1c", [8, 8], BF16)
        I["fm3"] = self.din("fm3", [128, 64 * 64], BF16)
        I["fm3c"] = self.din("fm3c", [128, 4 * 64], BF16)
        self.I = I
        self.out = nc.dram_tensor("out", [L, D], F32, kind="ExternalOutput").ap()
        S = {}
        S["hT"] = self.dscr("hT", [D, NT], F32)
        S["UVd"] = self.dscr("UVd", [NT, 512], BF16)
        S["QTd"] = self.dscr("QTd", [96, 4 * NT], BF16)
        S["KTd"] = self.dscr("KTd", [96, 4 * NT], BF16)
        S["Vd"] = self.dscr("Vd", [NT, 256], BF16)
        S["uTd"] = self.dscr("uTd", [256, NT], BF16)
        S["vd"] = self.dscr("vd", [NT, 256], BF16)
        S["pd"] = self.dscr("pd", [NT, 256], BF16)
        S["mixT"] = self.dscr("mixT", [D, NT], BF16)
        S["Ad"] = self.dscr("Ad", [2 * 64 * 64 * 256], BF16)
        self.S = S
        self.B = {k: Buf(k, multi=True) for k in S}
        self.B["out"] = Buf("out", multi=True)

        with ExitStack() as top:
            K = self.K = Ctx(nc, top)
            self.ps = []
            self.psb = []
            for i in range(8):
                self.ps.append(top.enter_context(nc.psum_tensor(f"psb{i}", [128, 512], F32)))
                self.psb.append(Buf(f"ps{i}"))
            self.vecs, self.b_vecs = self.sb(top, "vecs", [128, 2 * NVL + 8], F32)
            self.ident, self.b_ident = self.sb(top, "ident", [128, 128], F32)
            self.ones, self.b_ones = self.sb(top, "ones", [128, 128], BF16)
            self.modv, self.b_modv = self.sb(top, "modv", [128, 2 * 2 * 72], F32)
            self.AB, self.b_AB = self.sb(top, "AB", [128, 2 * 3 * 2 * 3 * 8], F32)
            K.dma(K.qs, lambda e: e.dma_start(out=self.vecs[:, :], in_=I["vecs"][:, :]), writes=[self.b_vecs])
            K.dma(K.qs, lambda e: e.dma_start(out=self.ident[:, :], in_=I["ident"][:, :]), writes=[self.b_ident])
            K.op(K.dve, [lambda: nc.vector.memset(self.ones[:, :], 1.0)], writes=[self.b_ones])

            stages = cfg.get("stages", None)

            def want(name):
                return stages is None or name in stages

            if want("pro"):
                self.stage_transpose_in()
                self.stage_ada(nl)
            for l in range(nl):
                last = l == 1
                if want(f"ffn1_{l}"):
                    self.stage_ffn(l, 0, with_ctx=True, pre_mix=False)
                if want(f"proj_{l}"):
                    self.stage_proj(l)
                if want(f"att_{l}") or want(f"dft_{l}"):
                    self.stage_attdft(l)
                if want(f"sgp_{l}"):
                    self.stage_sgp(l)
                if want(f"ffn2_{l}"):
                    self.stage_ffn(l, 2, with_ctx=not last, pre_mix=True)
            if want("fin"):
                self.stage_final()
            K.barrier()
        return nc

    def ab(self, l, sub, which, kind):
        o = (((l * 3 + sub) * 2 + which) * 3 + kind) * 8
        return self.AB[:, o:o + 8]

    def vcol(self, l, off, n=1):
        o = l * NVL + off
        return self.vecs[:, o:o + n]

    def stage_transpose_in(self):
        nc, K, I = self.nc, self.K, self.I
        hTv = self.S["hT"].rearrange("(c p) t -> p c t", p=128)
        with ExitStack() as st:
            xin = [self.sb(st, f"xin{i}", [128, 4, D], F32) for i in range(2)]
            xT = [self.sb(st, f"xT{i}", [128, 8, 512], F32) for i in range(2)]
            groups = [(I["x"], g * 512, 4, g * 512) for g in range(8)] + [(I["ctx"], 0, 2, L)]
            for gi, (src, r0, nk, c0) in enumerate(groups):
                xi, bxi = xin[gi % 2]
                xo, bxo = xT[gi % 2]
                srcv = src[r0:r0 + nk * 128, :].rearrange("(k p) d -> p k d", p=128)
                K.dma(K.qs, lambda e, xi=xi, srcv=srcv, nk=nk: e.dma_start(out=xi[:, 0:nk, :], in_=srcv), writes=[bxi])
                for c in range(8):
                    pb = c
                    fns = [
                        (lambda k=k, c=c, pb=pb, xi=xi: nc.tensor.transpose(self.ps[pb][:, k * 128:(k + 1) * 128], xi[:, k, c * 128:(c + 1) * 128], self.ident[:, :]))
                        for k in range(nk)
                    ]
                    K.op(K.pe, fns, reads=[bxi, self.b_ident], writes=[self.psb[pb]])
                    if c % 2 == 0:
                        K.op(K.act, [lambda c=c, pb=pb, xo=xo, nk=nk: nc.scalar.copy(out=xo[:, c, 0:nk * 128], in_=self.ps[pb][:, 0:nk * 128])],
                             reads=[self.psb[pb]], writes=[bxo])
                    else:
                        K.op(K.dve, [lambda c=c, pb=pb, xo=xo, nk=nk: nc.vector.tensor_copy(out=xo[:, c, 0:nk * 128], in_=self.ps[pb][:, 0:nk * 128])],
                             reads=[self.psb[pb]], writes=[bxo])
                K.dma(K.qs, lambda e, xo=xo, c0=c0, nk=nk: e.dma_start(out=hTv[:, :, c0:c0 + nk * 128], in_=xo[:, :, 0:nk * 128]),
                      reads=[bxo], writes=[self.B["hT"]])
            K.barrier()

    def stage_ada(self, nl):
        nc, K, I = self.nc, self.K, self.I
        with ExitStack() as st:
            cc, bcc = self.sb(st, "cc", [128, 16], F32)
            sc, bsc = self.sb(st, "sc", [128, 16], F32)
            wa = [self.sb(st, f"wa{i}", [128, 8, 512], F32) for i in range(3)]
            mrow, bmrow = self.sb(st, "mrow", [2, 9 * D], F32)
            K.dma(K.qs, lambda e: e.dma_start(out=cc[:, :], in_=I["cc"][:, :]), writes=[bcc])
            K.op(K.act, [lambda: nc.scalar.activation(out=sc[:, :], in_=cc[:, :], func=AF.Silu)], reads=[bcc], writes=[bsc])
            scv = sc[:, :].rearrange("p (k w) -> p k w", w=2)
            for l in range(nl):
                wv = I["wada"][l].rearrange("(k p) n -> p k n", p=128)
                pb = 6 + l
                for grp in range(18):
                    w, bw = wa[grp % 3]
                    K.dma(K.qs, lambda e, w=w, grp=grp, wv=wv: e.dma_start(out=w[:, :, :], in_=wv[:, :, grp * 512:(grp + 1) * 512]), writes=[bw])
                    rb = grp % 4
                    fns = [(lambda k=k, w=w, rb=rb: nc.tensor.matmul(self.ps[rb][0:2, :], lhsT=scv[:, k, :], rhs=w[:, k, :], start=(k == 0), stop=(k == 7))) for k in range(8)]
                    K.op(K.pe, fns, reads=[bw, bsc], writes=[self.psb[rb]])
                    K.op(K.act, [lambda grp=grp, rb=rb: nc.scalar.copy(out=mrow[0:2, grp * 512:(grp + 1) * 512], in_=self.ps[rb][0:2, :])], reads=[self.psb[rb]], writes=[bmrow])
                fns = [(lambda c=c, pb=pb: nc.tensor.transpose(self.ps[pb][:, 2 * c:2 * c + 2], mrow[0:2, c * 128:(c + 1) * 128], self.ident[0:2, 0:2])) for c in range(72)]
                K.op(K.pe, fns, reads=[bmrow, self.b_ident], writes=[self.psb[pb]])
                psv = self.ps[pb][:, 0:144].rearrange("p (c w) -> p c w", w=2)
                for wch in range(2):
                    o = (l * 2 + wch) * 72
                    K.op(K.dve, [lambda o=o, wch=wch, psv=psv, l=l: nc.vector.tensor_tensor(
                        out=self.modv[:, o:o + 72], in0=psv[:, :, wch], in1=self.vcol(l, V_BADA, 72), op=ALU.add)],
                        reads=[self.psb[pb], self.b_vecs], writes=[self.b_modv])
                for sub, (gcol, gate_mul) in enumerate(((V_GF1, 0.5), (V_GMIX, 1.0), (V_GF2, 0.5))):
                    for wch in range(2):
                        o = (l * 2 + wch) * 72
                        sh = self.modv[:, o + (3 * sub) * 8: o + (3 * sub) * 8 + 8]
                        scl = self.modv[:, o + (3 * sub + 1) * 8: o + (3 * sub + 1) * 8 + 8]
                        gt = self.modv[:, o + (3 * sub + 2) * 8: o + (3 * sub + 2) * 8 + 8]
                        K.op(K.dve, [lambda scl=scl, l=l, sub=sub, wch=wch, gcol=gcol: nc.vector.scalar_tensor_tensor(
                            out=self.ab(l, sub, wch, 0), in0=scl, scalar=1.0, in1=self.vcol(l, gcol, 8), op0=ALU.add, op1=ALU.mult)],
                            reads=[self.b_modv, self.b_vecs], writes=[self.b_AB])
                        K.op(K.dve, [lambda sh=sh, l=l, sub=sub, wch=wch: nc.vector.tensor_copy(out=self.ab(l, sub, wch, 1), in_=sh)],
                             reads=[self.b_modv], writes=[self.b_AB])
                        K.op(K.dve, [lambda gt=gt, l=l, sub=sub, wch=wch, gate_mul=gate_mul: nc.vector.tensor_scalar(
                            out=self.ab(l, sub, wch, 2), in0=gt, scalar1=gate_mul, scalar2=None, op0=ALU.mult)],
                            reads=[self.b_modv], writes=[self.b_AB])
            K.barrier()

    def rms_modulate(self, xt, bxt, xn, bxn, rs, brs, tmps, segs, subs, l, sub, banks):
        nc, K = self.nc, self.K
        T = sum(w for _, w, _ in segs)
        K.op(K.act, [lambda: nc.scalar.activation(out=xn[:, :, 0:T], in_=xt[:, :, 0:T], func=AF.Square)], reads=[bxt], writes=[bxn])
        for si, (c0, w) in enumerate(subs):
            pb = banks[si]
            fns = [(lambda k=k, pb=pb, c0=c0, w=w: nc.tensor.matmul(self.ps[pb][:, 0:w], lhsT=self.ones[:, :], rhs=xn[:, k, c0:c0 + w], start=(k == 0), stop=(k == 7)))
                   for k in range(8)]
            K.op(K.pe, fns, reads=[bxn, self.b_ones], writes=[self.psb[pb]])
            K.op(K.act, [lambda pb=pb, c0=c0, w=w: nc.scalar.activation(out=rs[:, c0:c0 + w], in_=self.ps[pb][:, 0:w], func=AF.Sqrt, bias=self.epsb[:, 0:1], scale=1.0 / D)],
                 reads=[self.psb[pb], self.b_eps], writes=[brs])
        K.op(K.dve, [lambda: nc.vector.reciprocal(out=rs[:, 0:T], in_=rs[:, 0:T])], reads=[brs], writes=[brs])
        for c in range(8):
            tm, btm = tmps[c % len(tmps)]
            for (c0, w, wch) in segs:
                K.op(K.dve, [lambda c=c, c0=c0, w=w, wch=wch, tm=tm: nc.vector.scalar_tensor_tensor(
                    out=tm[:, c0:c0 + w], in0=xt[:, c, c0:c0 + w], scalar=self.ab(l, sub, wch, 0)[:, c:c + 1], in1=rs[:, c0:c0 + w],
                    op0=ALU.mult, op1=ALU.mult)], reads=[bxt, brs, self.b_AB], writes=[btm])
            for (c0, w, wch) in segs:
                K.op(K.act, [lambda c=c, c0=c0, w=w, wch=wch, tm=tm: nc.scalar.activation(
                    out=xn[:, c, c0:c0 + w], in_=tm[:, c0:c0 + w], func=AF.Identity, bias=self.ab(l, sub, wch, 1)[:, c:c + 1], scale=1.0)],
                    reads=[btm, self.b_AB], writes=[bxn])

    def ensure_eps(self, st):
        nc, K = self.nc, self.K
        self.epsb, self.b_eps = self.sb(st, "eps", [128, 1], F32)
        K.op(K.dve, [lambda: nc.vector.memset(self.epsb[:, :], EPS)], writes=[self.b_eps])

    def stage_ffn(self, l, sub, with_ctx, pre_mix):
        nc, K, I, S = self.nc, self.K, self.I, self.S
        T = 1088 if with_ctx else 1024
        segs = [(0, 1024, 0)] + ([(1024, 64, 1)] if with_ctx else [])
        subs = [(0, 512), (512, 512)] + ([(1024, 64)] if with_ctx else [])
        nS = len(subs)
        w13 = (I["w13a"] if sub == 0 else I["w13b"])[l].rearrange("(k p) n -> p k n", p=128)
        w2 = (I["w2a"] if sub == 0 else I["w2b"])[l].rearrange("(j p) d -> p j d", p=128)
        wov = I["wout"][l].rearrange("(k p) n -> p k n", p=128)
        hTv = S["hT"].rearrange("(c p) t -> p c t", p=128)
        mixv = S["mixT"].rearrange("(c p) t -> p c t", p=128)
        ps, psb = self.ps, self.psb
        with ExitStack() as st:
            self.ensure_eps(st)
            xts = [self.sb(st, f"xt{i}", [128, 8, T], F32) for i in range(2)]
            xn, bxn = self.sb(st, "xn", [128, 8, T], BF16)
            rs, brs = self.sb(st, "rs", [128, T], F32)
            tm, btm = self.sb(st, "tmp", [128, T], F32)
            sas = [self.sb(st, f"sa{i}", [128, T], BF16) for i in range(2)]
            g, bg = self.sb(st, "g", [128, NJ, T], BF16)
            wab = [(self.sb(st, f"w13a{i}", [128, 8, 256], BF16), self.sb(st, f"w13b{i}", [128, 8, 256], BF16)) for i in range(2)]
            w2r = [self.sb(st, f"w2r{i}", [128, NJ, 256], BF16) for i in range(2)]
            wor = [self.sb(st, f"wor{i}", [128, 8, 256], BF16) for i in range(2)] if pre_mix else None
            Abanks, Bbanks = (0, 1, 2), (3, 4, 5)
            cnt = {"w13": 0, "w2": 0, "wo": 0}

            def load(t):
                xt, bxt = xts[t % 2]
                K.dma(K.qs, lambda e: e.dma_start(out=xt[:, :, 0:1024], in_=hTv[:, :, t * 1024:(t + 1) * 1024]), reads=[self.B["hT"]], writes=[bxt])
                if with_ctx:
                    K.dma(K.qs, lambda e: e.dma_start(out=xt[:, :, 1024:1088], in_=hTv[:, :, L + t * 64:L + (t + 1) * 64]), reads=[self.B["hT"]], writes=[bxt])

            def premix(t):
                xt, bxt = xts[t % 2]
                K.dma(K.qs, lambda e: e.dma_start(out=xn[:, :, 0:1024], in_=mixv[:, :, t * 1024:(t + 1) * 1024]), reads=[self.B["mixT"]], writes=[bxn])
                if with_ctx:
                    K.dma(K.qs, lambda e: e.dma_start(out=xn[:, :, 1024:1088], in_=mixv[:, :, L + t * 64:L + (t + 1) * 64]), reads=[self.B["mixT"]], writes=[bxn])
                for i2 in range(4):
                    wo, bwo = wor[cnt["wo"] % 2]
                    cnt["wo"] += 1
                    K.dma(K.qg, lambda e, wo=wo, i2=i2: e.dma_start(out=wo[:, :, :], in_=wov[:, :, i2 * 256:(i2 + 1) * 256]), writes=[bwo])
                    for ii in range(2):
                        i = i2 * 2 + ii
                        banks = Abanks if i % 2 == 0 else Bbanks
                        fns = []
                        for k in range(8):
                            for si, (c0, w) in enumerate(subs):
                                fns.append(lambda ii=ii, k=k, c0=c0, w=w, pb=banks[si], wo=wo: nc.tensor.matmul(
                                    ps[pb][:, 0:w], lhsT=wo[:, k, ii * 128:(ii + 1) * 128], rhs=xn[:, k, c0:c0 + w], start=(k == 0), stop=(k == 7)))
                        K.op(K.pe, fns, reads=[bwo, bxn], writes=[psb[b_] for b_ in banks[:nS]])
                        for si, (c0, w) in enumerate(subs):
                            wch = 0 if c0 < 1024 else 1
                            K.op(K.dve, [lambda i=i, c0=c0, w=w, wch=wch, pb=banks[si]: nc.vector.scalar_tensor_tensor(
                                out=xt[:, i, c0:c0 + w], in0=ps[pb][:, 0:w], scalar=self.ab(l, 1, wch, 2)[:, i:i + 1], in1=xt[:, i, c0:c0 + w],
                                op0=ALU.mult, op1=ALU.add)], reads=[psb[banks[si]], self.b_AB, bxt], writes=[bxt])

            def norm1(t):
                xt, bxt = xts[t % 2]
                K.op(K.act, [lambda: nc.scalar.activation(out=xn[:, :, 0:T], in_=xt[:, :, 0:T], func=AF.Square)], reads=[bxt], writes=[bxn])
                nbanks = (6, 7, 6)
                for si, (c0, w) in enumerate(subs):
                    pb = nbanks[si]
                    fns = [(lambda k=k, pb=pb, c0=c0, w=w: nc.tensor.matmul(ps[pb][:, 0:w], lhsT=self.ones[:, :], rhs=xn[:, k, c0:c0 + w], start=(k == 0), stop=(k == 7)))
                           for k in range(8)]
                    K.op(K.pe, fns, reads=[bxn, self.b_ones], writes=[psb[pb]])
                    K.op(K.act, [lambda pb=pb, c0=c0, w=w: nc.scalar.activation(out=rs[:, c0:c0 + w], in_=ps[pb][:, 0:w], func=AF.Sqrt, bias=self.epsb[:, 0:1], scale=1.0 / D)],
                         reads=[psb[pb], self.b_eps], writes=[brs])
                K.op(K.dve, [lambda: nc.vector.reciprocal(out=rs[:, 0:T], in_=rs[:, 0:T])], reads=[brs], writes=[brs])

            def norm2(t):
                xt, bxt = xts[t % 2]
                for c in range(8):
                    for (c0, w, wch) in segs:
                        K.op(K.dve, [lambda c=c, c0=c0, w=w, wch=wch: nc.vector.scalar_tensor_tensor(
                            out=tm[:, c0:c0 + w], in0=xt[:, c, c0:c0 + w], scalar=self.ab(l, sub, wch, 0)[:, c:c + 1], in1=rs[:, c0:c0 + w],
                            op0=ALU.mult, op1=ALU.mult)], reads=[bxt, brs, self.b_AB], writes=[btm])
                    for (c0, w, wch) in segs:
                        K.op(K.act, [lambda c=c, c0=c0, w=w, wch=wch: nc.scalar.activation(
                            out=xn[:, c, c0:c0 + w], in_=tm[:, c0:c0 + w], func=AF.Identity, bias=self.ab(l, sub, wch, 1)[:, c:c + 1], scale=1.0)],
                            reads=[btm, self.b_AB], writes=[bxn])

            def phase1(t):
                for gi in range(NJ // 2):
                    j0 = gi * 2
                    (wa, bwa), (wb, bwb) = wab[cnt["w13"] % 2]
                    cnt["w13"] += 1
                    K.dma(K.qg, lambda e, wa=wa, j0=j0: e.dma_start(out=wa[:, :, :], in_=w13[:, :, j0 * 128:(j0 + 2) * 128]), writes=[bwa])
                    K.dma(K.qg, lambda e, wb=wb, j0=j0: e.dma_start(out=wb[:, :, :], in_=w13[:, :, DFF + j0 * 128:DFF + (j0 + 2) * 128]), writes=[bwb])
                    for jj in range(2):
                        j = j0 + jj
                        sa, bsa = sas[j % 2]
                        fns = []
                        for k in range(8):
                            for si, (c0, w) in enumerate(subs):
                                fns.append(lambda jj=jj, k=k, c0=c0, w=w, pb=Abanks[si], wa=wa: nc.tensor.matmul(
                                    ps[pb][:, 0:w], lhsT=wa[:, k, jj * 128:(jj + 1) * 128], rhs=xn[:, k, c0:c0 + w], start=(k == 0), stop=(k == 7)))
                        K.op(K.pe, fns, reads=[bwa, bxn], writes=[psb[b_] for b_ in Abanks[:nS]])
                        for si, (c0, w) in enumerate(subs):
                            K.op(K.act, [lambda c0=c0, w=w, pb=Abanks[si], sa=sa: nc.scalar.activation(out=sa[:, c0:c0 + w], in_=ps[pb][:, 0:w], func=AF.Silu)],
                                 reads=[psb[Abanks[si]]], writes=[bsa])
                        fns = []
                        for k in range(8):
                            for si, (c0, w) in enumerate(subs):
                                fns.append(lambda jj=jj, k=k, c0=c0, w=w, pb=Bbanks[si], wb=wb: nc.tensor.matmul(
                                    ps[pb][:, 0:w], lhsT=wb[:, k, jj * 128:(jj + 1) * 128], rhs=xn[:, k, c0:c0 + w], start=(k == 0), stop=(k == 7)))
                        K.op(K.pe, fns, reads=[bwb, bxn], writes=[psb[b_] for b_ in Bbanks[:nS]])
                        for si, (c0, w) in enumerate(subs):
                            K.op(K.dve, [lambda j=j, c0=c0, w=w, pb=Bbanks[si], sa=sa: nc.vector.tensor_tensor(
                                out=g[:, j, c0:c0 + w], in0=ps[pb][:, 0:w], in1=sa[:, c0:c0 + w], op=ALU.mult)],
                                reads=[psb[Bbanks[si]], bsa], writes=[bg])

            def phase2(t, i2s):
                xt, bxt = xts[t % 2]
                for i2 in i2s:
                    wr, bwr = w2r[cnt["w2"] % 2]
                    cnt["w2"] += 1
                    K.dma(K.qg, lambda e, wr=wr, i2=i2: e.dma_start(out=wr[:, :, :], in_=w2[:, :, i2 * 256:(i2 + 1) * 256]), writes=[bwr])
                    for ii in range(2):
                        i = i2 * 2 + ii
                        banks = Abanks if i % 2 == 0 else Bbanks
                        fns = []
                        for j in range(NJ):
                            for si, (c0, w) in enumerate(subs):
                                fns.append(lambda ii=ii, j=j, c0=c0, w=w, pb=banks[si], wr=wr: nc.tensor.matmul(
                                    ps[pb][:, 0:w], lhsT=wr[:, j, ii * 128:(ii + 1) * 128], rhs=g[:, j, c0:c0 + w], start=(j == 0), stop=(j == NJ - 1)))
                        K.op(K.pe, fns, reads=[bwr, bg], writes=[psb[b_] for b_ in banks[:nS]])
                        for si, (c0, w) in enumerate(subs):
                            wch = 0 if c0 < 1024 else 1
                            K.op(K.dve, [lambda i=i, c0=c0, w=w, wch=wch, pb=banks[si]: nc.vector.scalar_tensor_tensor(
                                out=xt[:, i, c0:c0 + w], in0=ps[pb][:, 0:w], scalar=self.ab(l, sub, wch, 2)[:, i:i + 1], in1=xt[:, i, c0:c0 + w],
                                op0=ALU.mult, op1=ALU.add)], reads=[psb[banks[si]], self.b_AB, bxt], writes=[bxt])

            def store(t):
                xt, bxt = xts[t % 2]
                K.dma(K.qs, lambda e: e.dma_start(out=hTv[:, :, t * 1024:(t + 1) * 1024], in_=xt[:, :, 0:1024]), reads=[bxt], writes=[self.B["hT"]])
                if with_ctx:
                    K.dma(K.qs, lambda e: e.dma_start(out=hTv[:, :, L + t * 64:L + (t + 1) * 64], in_=xt[:, :, 1024:1088]), reads=[bxt], writes=[self.B["hT"]])

            load(0)
            if pre_mix:
                premix(0)
            norm1(0)
            norm2(0)
            for t in range(4):
                nxt = t + 1 < 4
                if nxt:
                    load(t + 1)
                phase1(t)
                phase2(t, [0])
                if nxt:
                    if pre_mix:
                        premix(t + 1)
                    norm1(t + 1)
                phase2(t, [1])
                if nxt:
                    norm2(t + 1)
                phase2(t, [2, 3])
                store(t)
            K.barrier()

    def stage_proj(self, l):
        nc, K, I, S = self.nc, self.K, self.I, self.S
        hTv = S["hT"].rearrange("(c p) t -> p c t", p=128)
        ropev = I["rope"].rearrange("p (w t) -> p w t", w=2)
        QTv = S["QTd"].rearrange("r (h t) -> r h t", h=4)
        KTv = S["KTd"].rearrange("r (h t) -> r h t", h=4)
        uTv = S["uTd"].rearrange("(c p) t -> p c t", p=128)
        with ExitStack() as st:
            self.ensure_eps(st)
            win, bwin = self.sb(st, "win", [128, 8, 1536], BF16)
            wuq, bwuq = self.sb(st, "wuq", [128, 2, 384], BF16)
            wuqp, bwuqp = self.sb(st, "wuqp", [128, 2, 384], BF16)
            wukv, bwukv = self.sb(st, "wukv", [128, 512], BF16)
            wv, bwv = self.sb(st, "wv", [128, 4, 64], BF16)
            wfn, bwfn = self.sb(st, "wfn", [64, 256], F32)
            c64, bc64 = self.sb(st, "c64", [64, 1024], F32)
            csw, bcsw = self.sb(st, "csw", [128, 2, 2, 512], BF16)
            xts = [self.sb(st, f"pxt{i}", [128, 8, 512], F32) for i in range(2)]
            rts = [self.sb(st, f"prt{i}", [128, 2, 512], F32) for i in range(2)]
            xns = [self.sb(st, f"pxn{i}", [128, 8, 512], BF16) for i in range(2)]
            rs, brs = self.sb(st, "prs", [128, 512], F32)
            tmps = [self.sb(st, f"ptmp{i}", [128, 512], F32) for i in range(2)]
            zF, bzF = self.sb(st, "zF", [128, 2, 512], BF16)
            UVt, bUVt = self.sb(st, "UVt", [128, 4, 512], BF16)
            sqa, bsqa = self.sb(st, "sqa", [128, 512], BF16)
            sqb, bsqb = self.sb(st, "sqb", [128, 512], BF16)
            sqc, bsqc = self.sb(st, "sqc", [128, 512], BF16)
            rq, brq = self.sb(st, "rq", [128, 512], F32)
            rkv, brkv = self.sb(st, "rkv", [128, 512], F32)
            cqa, bcqa = self.sb(st, "cqa", [128, 512], BF16)
            cqb, bcqb = self.sb(st, "cqb", [128, 512], BF16)
            ckvn, bckvn = self.sb(st, "ckvn", [128, 512], BF16)
            t1, bt1 = self.sb(st, "t1", [128, 512], F32)
            t2, bt2 = self.sb(st, "t2", [128, 512], F32)
            QTt, bQTt = self.sb(st, "QTt", [128, 4, 512], BF16)
            KTt, bKTt = self.sb(st, "KTt", [128, 4, 512], BF16)
            Vt, bVt = self.sb(st, "Vt", [128, 4, 256], BF16)
            uTt, buTt = self.sb(st, "uTt", [128, 2, 512], BF16)
            v_t, bv_t = self.sb(st, "v_t", [128, 4, 256], BF16)
            p_t, bp_t = self.sb(st, "p_t", [128, 4, 256], BF16)
            gv, bgv = self.sb(st, "gv", [128, 4, 256], F32)
            sqv, bsqv = self.sb(st, "sqv", [128, 4, 256], F32)
            ssv, bssv = self.sb(st, "ssv", [128, 16], F32)
            ps, psb = self.ps, self.psb
            K.dma(K.qg, lambda e: e.dma_start(out=win[:, :, :], in_=I["winx"][l].rearrange("(k p) n -> p k n", p=128)), writes=[bwin])
            for (dst, bdst, src) in ((wuq, bwuq, I["wuq"]), (wuqp, bwuqp, I["wuqp"])):
                K.dma(K.qg, lambda e, dst=dst, src=src: e.dma_start(out=dst[:, 0, :], in_=src[l, 0:128, :]), writes=[bdst])
                K.dma(K.qg, lambda e, dst=dst, src=src: e.dma_start(out=dst[0:64, 1, :], in_=src[l, 128:192, :]), writes=[bdst])
            K.dma(K.qg, lambda e: e.dma_start(out=wukv[:, :], in_=I["wukv"][l]), writes=[bwukv])
            K.dma(K.qg, lambda e: e.dma_start(out=wv[:, :, :], in_=I["wukv"][l].rearrange("r (h two c) -> r h two c", two=2, c=64)[:, :, 1, :]), writes=[bwv])
            K.dma(K.qs, lambda e: e.dma_start(out=wfn[:, :], in_=I["wfn"][l]), writes=[bwfn])
            K.dma(K.qs, lambda e: e.dma_start(out=c64[:, :], in_=I["c64"][:, :]), writes=[bc64])
            K.op(K.dve, [lambda: nc.vector.memset(csw[:, :, :, :], 0.0)], writes=[bcsw])
            for cq_ in range(2):
                for lc in range(2):
                    pb = cq_ * 2 + lc
                    fns = []
                    for part, kind in ((0, 2 * lc), (1, 2 * lc + 1)):
                        for hh in range(2):
                            h = 2 * cq_ + hh
                            co = part * 256 + h * 64
                            ko = (kind * 2 + hh) * 128
                            fns.append(lambda co=co, ko=ko, h=h, pb=pb: nc.tensor.matmul(
                                ps[pb][:, co:co + 64], lhsT=c64[:, ko:ko + 128], rhs=wfn[:, h * 64:(h + 1) * 64], start=True, stop=True))
                    K.op(K.pe, fns, reads=[bc64, bwfn], writes=[psb[pb]])
                    for part in range(2):
                        co = part * 256 + cq_ * 128
                        K.op(K.dve, [lambda co=co, pb=pb, cq_=cq_, lc=lc: nc.vector.tensor_copy(out=csw[:, cq_, lc, co:co + 128], in_=ps[pb][:, co:co + 128])],
                             reads=[psb[pb]], writes=[bcsw])
            tiles = [(t * 512, 512, 0) for t in range(8)] + [(L, 256, 1)]
            def pload(ti):
                c0, T, wch = tiles[ti]
                xt, bxt = xts[ti % 2]
                rt, brt = rts[ti % 2]
                K.dma(K.qs, lambda e: e.dma_start(out=xt[:, :, 0:T], in_=hTv[:, :, c0:c0 + T]), reads=[self.B["hT"]], writes=[bxt])
                K.dma(K.qs, lambda e: e.dma_start(out=rt[:, :, 0:T], in_=ropev[:, :, c0:c0 + T]), writes=[brt])

            def pnorm1(ti):
                c0, T, wch = tiles[ti]
                xt, bxt = xts[ti % 2]
                xn, bxn = xns[ti % 2]
                K.op(K.act, [lambda: nc.scalar.activation(out=xn[:, :, 0:T], in_=xt[:, :, 0:T], func=AF.Square)], reads=[bxt], writes=[bxn])
                fns = [(lambda k=k: nc.tensor.matmul(ps[7][:, 0:T], lhsT=self.ones[:, :], rhs=xn[:, k, 0:T], start=(k == 0), stop=(k == 7))) for k in range(8)]
                K.op(K.pe, fns, reads=[bxn, self.b_ones], writes=[psb[7]])
                K.op(K.act, [lambda: nc.scalar.activation(out=rs[:, 0:T], in_=ps[7][:, 0:T], func=AF.Sqrt, bias=self.epsb[:, 0:1], scale=1.0 / D)],
                     reads=[psb[7], self.b_eps], writes=[brs])
                K.op(K.dve, [lambda: nc.vector.reciprocal(out=rs[:, 0:T], in_=rs[:, 0:T])], reads=[brs], writes=[brs])

            def pnorm2(ti):
                c0, T, wch = tiles[ti]
                xt, bxt = xts[ti % 2]
                xn, bxn = xns[ti % 2]
                for c in range(8):
                    tm, btm = tmps[c % 2]
                    K.op(K.dve, [lambda c=c, tm=tm: nc.vector.scalar_tensor_tensor(
                        out=tm[:, 0:T], in0=xt[:, c, 0:T], scalar=self.ab(l, 1, wch, 0)[:, c:c + 1], in1=rs[:, 0:T], op0=ALU.mult, op1=ALU.mult)],
                        reads=[bxt, brs, self.b_AB], writes=[btm])
                    K.op(K.act, [lambda c=c, tm=tm: nc.scalar.activation(
                        out=xn[:, c, 0:T], in_=tm[:, 0:T], func=AF.Identity, bias=self.ab(l, 1, wch, 1)[:, c:c + 1], scale=1.0)],
                        reads=[btm, self.b_AB], writes=[bxn])

            pload(0)
            pnorm1(0)
            pnorm2(0)
            for ti, (c0, T, wch) in enumerate(tiles):
                ntc = T // 128
                xt, bxt = xts[ti % 2]
                rt, brt = rts[ti % 2]
                xn, bxn = xns[ti % 2]
                has_next = ti + 1 < len(tiles)
                if has_next:
                    pload(ti + 1)

                def fm_block(blk, pb):
                    fns = [(lambda k=k: nc.tensor.matmul(ps[pb][:, 0:T], lhsT=win[:, k, blk * 128:(blk + 1) * 128], rhs=xn[:, k, 0:T], start=(k == 0), stop=(k == 7)))
                           for k in range(8)]
                    K.op(K.pe, fns, reads=[bwin, bxn], writes=[psb[pb]])

                fm_block(2, 2)
                fm_block(3, 3)
                fm_block(4, 4)
                K.op(K.act, [lambda: nc.scalar.activation(out=sqa[:, 0:T], in_=ps[2][:, 0:T], func=AF.Square)], reads=[psb[2]], writes=[bsqa])
                K.op(K.act, [lambda: nc.scalar.activation(out=sqb[0:64, 0:T], in_=ps[3][0:64, 0:T], func=AF.Square)], reads=[psb[3]], writes=[bsqb])
                K.op(K.act, [lambda: nc.scalar.activation(out=sqc[:, 0:T], in_=ps[4][:, 0:T], func=AF.Square)], reads=[psb[4]], writes=[bsqc])
                fm_block(0, 0)
                fm_block(1, 1)
                K.op(K.act, [lambda: nc.scalar.copy(out=zF[:, 0, 0:T], in_=ps[0][:, 0:T])], reads=[psb[0]], writes=[bzF])
                K.op(K.act, [lambda: nc.scalar.copy(out=zF[:, 1, 0:T], in_=ps[1][:, 0:T])], reads=[psb[1]], writes=[bzF])
                K.op(K.pe, [lambda: nc.tensor.matmul(ps[5][:, 0:T], lhsT=self.ones[:, :], rhs=sqa[:, 0:T], start=True, stop=False),
                            lambda: nc.tensor.matmul(ps[5][:, 0:T], lhsT=self.ones[0:64, :], rhs=sqb[0:64, 0:T], start=False, stop=True)],
                     reads=[bsqa, bsqb, self.b_ones], writes=[psb[5]])
                K.op(K.pe, [lambda: nc.tensor.matmul(ps[6][:, 0:T], lhsT=self.ones[:, :], rhs=sqc[:, 0:T], start=True, stop=True)],
                     reads=[bsqc, self.b_ones], writes=[psb[6]])
                K.op(K.act, [lambda: nc.scalar.activation(out=rq[:, 0:T], in_=ps[5][:, 0:T], func=AF.Sqrt, bias=self.epsb[:, 0:1], scale=1.0 / 192)],
                     reads=[psb[5], self.b_eps], writes=[brq])
                K.op(K.act, [lambda: nc.scalar.activation(out=rkv[:, 0:T], in_=ps[6][:, 0:T], func=AF.Sqrt, bias=self.epsb[:, 0:1], scale=1.0 / 128)],
                     reads=[psb[6], self.b_eps], writes=[brkv])
                K.op(K.dve, [lambda: nc.vector.reciprocal(out=rq[:, 0:T], in_=rq[:, 0:T])], reads=[brq], writes=[brq])
                K.op(K.dve, [lambda: nc.vector.reciprocal(out=rkv[:, 0:T], in_=rkv[:, 0:T])], reads=[brkv], writes=[brkv])
                K.op(K.dve, [lambda: nc.vector.scalar_tensor_tensor(out=cqa[:, 0:T], in0=ps[2][:, 0:T], scalar=self.vcol(l, V_GQ0), in1=rq[:, 0:T], op0=ALU.mult, op1=ALU.mult)],
                     reads=[psb[2], brq, self.b_vecs], writes=[bcqa])
                K.op(K.dve, [lambda: nc.vector.scalar_tensor_tensor(out=cqb[0:64, 0:T], in0=ps[3][0:64, 0:T], scalar=self.vcol(l, V_GQ1)[0:64, :], in1=rq[0:64, 0:T], op0=ALU.mult, op1=ALU.mult)],
                     reads=[psb[3], brq, self.b_vecs], writes=[bcqb])
                K.op(K.dve, [lambda: nc.vector.scalar_tensor_tensor(out=ckvn[:, 0:T], in0=ps[4][:, 0:T], scalar=self.vcol(l, V_GKV), in1=rkv[:, 0:T], op0=ALU.mult, op1=ALU.mult)],
                     reads=[psb[4], brkv, self.b_vecs], writes=[bckvn])
                fm_block(5, 5)
                if has_next:
                    pnorm1(ti + 1)
                for tc in range(ntc):
                    pb = tc % 2
                    fns = [(lambda cq_=cq_, tc=tc, pb=pb: nc.tensor.matmul(ps[pb][:, :], lhsT=zF[:, cq_, tc * 128:(tc + 1) * 128], rhs=csw[:, cq_, wch, :], start=(cq_ == 0), stop=(cq_ == 1)))
                           for cq_ in range(2)]
                    K.op(K.pe, fns, reads=[bzF, bcsw], writes=[psb[pb]])
                    K.op(K.act, [lambda tc=tc, pb=pb: nc.scalar.copy(out=UVt[:, tc, :], in_=ps[pb][:, :])], reads=[psb[pb]], writes=[bUVt])
                K.dma(K.qs, lambda e, c0=c0, T=T, ntc=ntc: e.dma_start(out=S["UVd"][c0:c0 + T, :].rearrange("(k p) n -> p k n", p=128), in_=UVt[:, 0:ntc, :]),
                      reads=[bUVt], writes=[self.B["UVd"]])
                K.op(K.dve, [lambda: nc.vector.tensor_tensor(out=t1[64:96, 0:T], in0=ps[3][64:96, 0:T], in1=rt[64:96, 0, 0:T], op=ALU.mult)], reads=[psb[3], brt], writes=[bt1])
                K.op(K.dve, [lambda: nc.vector.tensor_tensor(out=t2[64:96, 0:T], in0=ps[5][64:96, 0:T], in1=rt[64:96, 1, 0:T], op=ALU.mult)], reads=[psb[5], brt], writes=[bt2])
                K.op(K.dve, [lambda: nc.vector.tensor_tensor(out=KTt[64:96, :, 0:T], in0=t1[64:96, 0:T].unsqueeze(1).broadcast_to([32, 4, T]),
                                                              in1=t2[64:96, 0:T].unsqueeze(1).broadcast_to([32, 4, T]), op=ALU.add)], reads=[bt1, bt2], writes=[bKTt])
                fm_block(6, 6)
                fm_block(7, 7)
                K.op(K.act, [lambda: nc.scalar.activation(out=uTt[:, 0, 0:T], in_=ps[6][:, 0:T], func=AF.Gelu_apprx_tanh)], reads=[psb[6]], writes=[buTt])
                K.op(K.act, [lambda: nc.scalar.activation(out=uTt[:, 1, 0:T], in_=ps[7][:, 0:T], func=AF.Gelu_apprx_tanh)], reads=[psb[7]], writes=[buTt])
                K.dma(K.qs, lambda e, c0=c0, T=T: e.dma_start(out=uTv[:, :, c0:c0 + T], in_=uTt[:, :, 0:T]), reads=[buTt], writes=[self.B["uTd"]])
                for tc in range(ntc):
                    pb = tc % 2
                    fns = [(lambda k=k, tc=tc, pb=pb: nc.tensor.matmul(ps[pb][:, :], lhsT=xn[:, k, tc * 128:(tc + 1) * 128], rhs=win[:, k, 1024:1536], start=(k == 0), stop=(k == 7)))
                           for k in range(8)]
                    K.op(K.pe, fns, reads=[bxn, bwin], writes=[psb[pb]])
                    K.op(K.act, [lambda pb=pb, tc=tc: nc.scalar.activation(out=gv[:, tc, :], in_=ps[pb][:, 0:256], func=AF.Gelu_apprx_tanh)], reads=[psb[pb]], writes=[bgv])
                    K.op(K.act, [lambda pb=pb, tc=tc: nc.scalar.copy(out=p_t[:, tc, :], in_=ps[pb][:, 256:512])], reads=[psb[pb]], writes=[bp_t])
                nh = ntc * 4
                K.op(K.dve, [lambda: nc.vector.tensor_tensor(out=sqv[:, 0:ntc, :], in0=gv[:, 0:ntc, :], in1=gv[:, 0:ntc, :], op=ALU.mult)], reads=[bgv], writes=[bsqv])
                K.op(K.dve, [lambda: nc.vector.tensor_reduce(out=ssv[:, 0:nh], in_=sqv[:, 0:ntc, :].rearrange("p t (h c) -> p (t h) c", c=64), axis=AX.X, op=ALU.add)],
                     reads=[bsqv], writes=[bssv])
                K.op(K.act, [lambda: nc.scalar.activation(out=ssv[:, 0:nh], in_=ssv[:, 0:nh], func=AF.Sqrt, bias=self.epsb[:, 0:1], scale=1.0 / 64)],
                     reads=[bssv, self.b_eps], writes=[bssv])
                K.op(K.dve, [lambda: nc.vector.reciprocal(out=ssv[:, 0:nh], in_=ssv[:, 0:nh])], reads=[bssv], writes=[bssv])
                K.op(K.dve, [lambda: nc.vector.tensor_tensor(out=sqv[:, 0:ntc, :], in0=gv[:, 0:ntc, :], in1=self.vcol(l, V_GSB, 256).unsqueeze(1).broadcast_to([128, ntc, 256]), op=ALU.mult)],
                     reads=[bgv, self.b_vecs, bsqv], writes=[bsqv])
                K.op(K.dve, [lambda: nc.vector.tensor_tensor(out=v_t[:, 0:ntc, :].rearrange("p t (h c) -> p (t h) c", c=64),
                                                              in0=sqv[:, 0:ntc, :].rearrange("p t (h c) -> p (t h) c", c=64),
                                                              in1=ssv[:, 0:nh].unsqueeze(2).broadcast_to([128, nh, 64]), op=ALU.mult)],
                     reads=[bsqv, bssv], writes=[bv_t])
                if has_next:
                    pnorm2(ti + 1)
                K.dma(K.qs, lambda e, c0=c0, T=T, ntc=ntc: e.dma_start(out=S["vd"][c0:c0 + T, :].rearrange("(k p) n -> p k n", p=128), in_=v_t[:, 0:ntc, :]), reads=[bv_t], writes=[self.B["vd"]])
                K.dma(K.qs, lambda e, c0=c0, T=T, ntc=ntc: e.dma_start(out=S["pd"][c0:c0 + T, :].rearrange("(k p) n -> p k n", p=128), in_=p_t[:, 0:ntc, :]), reads=[bp_t], writes=[self.B["pd"]])
                for h in range(4):
                    pa, pq = (2, 3) if h % 2 == 0 else (4, 5)
                    for (pb, w_) in ((pa, wuq), (pq, wuqp)):
                        K.op(K.pe, [lambda pb=pb, w_=w_, h=h: nc.tensor.matmul(ps[pb][0:96, 0:T], lhsT=w_[:, 0, h * 96:(h + 1) * 96], rhs=cqa[:, 0:T], start=True, stop=False),
                                    lambda pb=pb, w_=w_, h=h: nc.tensor.matmul(ps[pb][0:96, 0:T], lhsT=w_[0:64, 1, h * 96:(h + 1) * 96], rhs=cqb[0:64, 0:T], start=False, stop=True)],
                             reads=[bwuq, bwuqp, bcqa, bcqb], writes=[psb[pb]])
                    K.op(K.act, [lambda h=h, pa=pa: nc.scalar.copy(out=QTt[0:64, h, 0:T], in_=ps[pa][0:64, 0:T])], reads=[psb[pa]], writes=[bQTt])
                    K.op(K.dve, [lambda pa=pa: nc.vector.tensor_tensor(out=t1[64:96, 0:T], in0=ps[pa][64:96, 0:T], in1=rt[64:96, 0, 0:T], op=ALU.mult)], reads=[psb[pa], brt], writes=[bt1])
                    K.op(K.dve, [lambda pq=pq: nc.vector.tensor_tensor(out=t2[64:96, 0:T], in0=ps[pq][64:96, 0:T], in1=rt[64:96, 1, 0:T], op=ALU.mult)], reads=[psb[pq], brt], writes=[bt2])
                    K.op(K.dve, [lambda h=h: nc.vector.tensor_tensor(out=QTt[64:96, h, 0:T], in0=t1[64:96, 0:T], in1=t2[64:96, 0:T], op=ALU.add)], reads=[bt1, bt2], writes=[bQTt])
                K.dma(K.qs, lambda e, c0=c0, T=T: e.dma_start(out=QTv[:, :, c0:c0 + T], in_=QTt[0:96, :, 0:T]), reads=[bQTt], writes=[self.B["QTd"]])
                for h in range(4):
                    pb = 6 + h % 2
                    K.op(K.pe, [lambda pb=pb, h=h: nc.tensor.matmul(ps[pb][0:64, 0:T], lhsT=wukv[:, h * 128:h * 128 + 64], rhs=ckvn[:, 0:T], start=True, stop=True)],
                         reads=[bwukv, bckvn], writes=[psb[pb]])
                    K.op(K.act, [lambda pb=pb, h=h: nc.scalar.copy(out=KTt[0:64, h, 0:T], in_=ps[pb][0:64, 0:T])], reads=[psb[pb]], writes=[bKTt])
                K.dma(K.qs, lambda e, c0=c0, T=T: e.dma_start(out=KTv[:, :, c0:c0 + T], in_=KTt[0:96, :, 0:T]), reads=[bKTt], writes=[self.B["KTd"]])
                for tc in range(ntc):
                    pb = tc % 2
                    K.op(K.pe, [lambda pb=pb, tc=tc: nc.tensor.matmul(ps[pb][:, 0:256], lhsT=ckvn[:, tc * 128:(tc + 1) * 128], rhs=wv[:, :, :], start=True, stop=True)],
                         reads=[bwv, bckvn], writes=[psb[pb]])
                    K.op(K.dve, [lambda pb=pb, tc=tc: nc.vector.tensor_copy(out=Vt[:, tc, :], in_=ps[pb][:, 0:256])], reads=[psb[pb]], writes=[bVt])
                K.dma(K.qs, lambda e, c0=c0, T=T, ntc=ntc: e.dma_start(out=S["Vd"][c0:c0 + T, :].rearrange("(k p) n -> p k n", p=128), in_=Vt[:, 0:ntc, :]), reads=[bVt], writes=[self.B["Vd"]])
            K.barrier()

    def gen_dft(self, l, st, banks, use_act):
        nc, K, I, S = self.nc, self.K, self.I, self.S
        mixv = S["mixT"].rearrange("(c p) t -> p c t", p=128)
        ps, psb = self.ps, self.psb
        X, bX = self.sb(st, "fX", [128, 64 * 256], BF16)
        A, bA = self.sb(st, "fA", [128, 64 * 256], BF16)
        A3, bA3 = X, bX
        Ft, bFt = self.sb(st, "fFt", [128, 2, L], BF16)
        M3, bM3 = self.sb(st, "fM3", [128, 64 * 64], BF16)
        W1, bW1 = self.sb(st, "fW1", [128, 128], BF16)
        cases = [(64, 0, L, "fw1", "fm3")]
        if l == 0:
            cases.append((4, L, CL, "fw1c", "fm3c"))
        bank = 0
        ev = 0
        for (R, base, Lx, w1n, m3n) in cases:
            R2 = 2 * R
            K.dma(K.qs, lambda e, w1n=w1n, R2=R2: e.dma_start(out=W1[0:R2, 0:R2], in_=I[w1n][:, :]), writes=[bW1])
            K.dma(K.qs, lambda e, m3n=m3n, R=R: e.dma_start(out=M3[:, 0:R * 64], in_=I[m3n][:, :]), writes=[bM3])
            for uv in range(2):
                K.dma(K.qs, lambda e, uv=uv, R=R, base=base, Lx=Lx: e.dma_start(
                    out=X[uv * R:(uv + 1) * R, :].rearrange("p (b d) -> p b d", d=256),
                    in_=S["UVd"][base:base + Lx, uv * 256:(uv + 1) * 256].rearrange("(a b) d -> a b d", b=64)),
                    reads=[self.B["UVd"]], writes=[bX])
            yield ("d", 3)
            for cb in range(32):
                pb = banks[bank % len(banks)]
                bank += 1
                K.op(K.pe, [lambda cb=cb, pb=pb, R2=R2: nc.tensor.matmul(ps[pb][0:R2, :], lhsT=W1[0:R2, 0:R2], rhs=X[0:R2, cb * 512:(cb + 1) * 512], start=True, stop=True)],
                     reads=[bW1, bX], writes=[psb[pb]])
                ev += 1
                if use_act and ev % 2 == 0:
                    K.op(K.act, [lambda cb=cb, pb=pb, R2=R2: nc.scalar.copy(out=A[0:R2, cb * 512:(cb + 1) * 512], in_=ps[pb][0:R2, :])], reads=[psb[pb]], writes=[bA])
                else:
                    K.op(K.dve, [lambda cb=cb, pb=pb, R2=R2: nc.vector.tensor_copy(out=A[0:R2, cb * 512:(cb + 1) * 512], in_=ps[pb][0:R2, :])], reads=[psb[pb]], writes=[bA])
                if cb % 4 == 3:
                    yield None
            Adv = S["Ad"][0:2 * 64 * R * 256].rearrange("(r a b d) -> r a b d", r=2, a=64, b=R)
            for ri in range(2):
                K.dma(K.qs, lambda e, ri=ri, R=R, Adv=Adv: e.dma_start(
                    out=Adv[ri].rearrange("a b d -> b a d"), in_=A[ri * R:(ri + 1) * R, :].rearrange("p (a d) -> p a d", d=256)),
                    reads=[bA], writes=[self.B["Ad"]])
            K.dma(K.qs, lambda e, R=R, Adv=Adv: e.dma_start(out=A3[:, 0:R * 256], in_=S["Ad"][0:2 * 64 * R * 256].rearrange("(q f) -> q f", q=128)),
                  reads=[self.B["Ad"]], writes=[bA3])
            yield ("d", 4)
            A3v = A3[:, 0:R * 256].rearrange("p (b d) -> p b d", d=256)
            M3v = M3[:, 0:R * 64].rearrange("p (b c) -> p b c", c=64)
            ng = min(8, R)
            for dc in range(2):
                Ftv = Ft[:, dc, 0:Lx].rearrange("p (b a) -> p a b", a=R)
                for j in range(R // ng):
                    pb = banks[bank % len(banks)]
                    bank += 1
                    fns = [(lambda i=i, j=j, dc=dc, pb=pb: nc.tensor.matmul(ps[pb][:, i * 64:(i + 1) * 64], lhsT=A3v[:, j * ng + i, dc * 128:(dc + 1) * 128], rhs=M3v[:, j * ng + i, :], start=True, stop=True))
                           for i in range(ng)]
                    K.op(K.pe, fns, reads=[bA3, bM3], writes=[psb[pb]])
                    src = ps[pb][:, 0:ng * 64].rearrange("p (a b) -> p a b", b=64)
                    ev += 1
                    if use_act and ev % 2 == 0:
                        K.op(K.act, [lambda j=j, Ftv=Ftv, src=src: nc.scalar.copy(out=Ftv[:, j * ng:(j + 1) * ng, :], in_=src)], reads=[psb[pb]], writes=[bFt])
                    else:
                        K.op(K.dve, [lambda j=j, Ftv=Ftv, src=src: nc.vector.tensor_copy(out=Ftv[:, j * ng:(j + 1) * ng, :], in_=src)], reads=[psb[pb]], writes=[bFt])
                    if j % 2 == 1:
                        yield None
            K.dma(K.qs, lambda e, base=base, Lx=Lx: e.dma_start(out=mixv[:, 0:2, base:base + Lx], in_=Ft[:, :, 0:Lx]), reads=[bFt], writes=[self.B["mixT"]])
            yield None

    def stage_attdft(self, l):
        with ExitStack() as st:
            ga = self.gen_att(l, st)
            gd = self.gen_dft(l, st, banks=(6, 7), use_act=False)
            hold = 0
            d_alive = True
            for _ in ga:
                if not d_alive:
                    continue
                if hold > 0:
                    hold -= 1
                    continue
                for _k in range(3):
                    try:
                        r = next(gd)
                    except StopIteration:
                        d_alive = False
                        break
                    if r is not None:
                        hold = r[1]
                        break
            if d_alive:
                for _ in gd:
                    pass
            self.K.barrier()

    def gen_att(self, l, st):
        nc, K, I, S = self.nc, self.K, self.I, self.S
        mixv = S["mixT"].rearrange("(c p) t -> p c t", p=128)
        QTv = S["QTd"].rearrange("r (h t) -> r h t", h=4)
        KTv = S["KTd"].rearrange("r (h t) -> r h t", h=4)
        ps, psb = self.ps, self.psb
        if True:
            KT, bKT = self.sb(st, "KT", [128, 4, NT], BF16)
            Vr, bVr = self.sb(st, "Vr", [128, 34, 512], BF16)
            Qts = [self.sb(st, f"Qt{i}", [128, 4, 512], BF16) for i in range(2)]
            Pts = [self.sb(st, f"Pt{i}", [128, 512], BF16) for i in range(4)]
            rec, brec = self.sb(st, "rec", [128, 512], F32)
            atts = [self.sb(st, f"att{i}", [128, 2, 512], BF16) for i in range(2)]
            K.dma(K.qs, lambda e: e.dma_start(out=KT[0:96, :, :], in_=KTv[:, :, :]), reads=[self.B["KTd"]], writes=[bKT])
            Vr4 = Vr[:, :, :].rearrange("p k (h c) -> p k h c", c=128)
            K.op(K.dve, [lambda: nc.vector.memset(Vr4[:, :, :, 64:128], 1.0)], writes=[bVr])
            Vdv = S["Vd"].rearrange("(k p) n -> p k n", p=128)
            for hh_ in range(4):
                K.dma(K.qs, lambda e, hh_=hh_: e.dma_start(out=Vr4[:, :, hh_, 0:64], in_=Vdv[:, :, hh_ * 64:(hh_ + 1) * 64]),
                      reads=[self.B["Vd"]], writes=[bVr])
            qtiles = [(n * 512, 512, list(range(34))) for n in range(8)]
            if l == 0:
                qtiles.append((L, 256, [32, 33]))
            pi = 0
            sbi = 0
            for qi, (c0, W, chunks) in enumerate(qtiles):
                Qt, bQt = Qts[qi % 2]
                at, bat = atts[qi % 2]
                K.dma(K.qs, lambda e, Qt=Qt, c0=c0, W=W: e.dma_start(out=Qt[0:96, :, 0:W], in_=QTv[:, :, c0:c0 + W]), reads=[self.B["QTd"]], writes=[bQt])
                for h in range(4):
                    ob = 4 + (h % 2)
                    vs = (h * 128, h * 128 + 128) if h % 2 == 0 else (h * 128 - 64, h * 128 + 64)
                    nk = len(chunks)
                    pend = []
                    for i in range(nk + 3):
                        if i < nk:
                            kc = chunks[i]
                            sbk = sbi % 4
                            sbi += 1
                            Pt, bPt = Pts[pi % 4]
                            pi += 1
                            K.op(K.pe, [lambda kc=kc, sbk=sbk, h=h, Qt=Qt, W=W: nc.tensor.matmul(ps[sbk][:, 0:W], lhsT=KT[0:96, h, kc * 128:(kc + 1) * 128], rhs=Qt[0:96, h, 0:W], start=True, stop=True)],
                                 reads=[bKT, bQt], writes=[psb[sbk]])
                            K.op(K.act, [lambda sbk=sbk, Pt=Pt, W=W: nc.scalar.activation(out=Pt[:, 0:W], in_=ps[sbk][:, 0:W], func=AF.Exp, scale=SCALE)],
                                 reads=[psb[sbk]], writes=[bPt])
                            pend.append((kc, Pt, bPt))
                        if i >= 3:
                            j = i - 3
                            kc, Pt, bPt = pend[j]
                            K.op(K.pe, [lambda kc=kc, Pt=Pt, ob=ob, vs=vs, W=W, j=j, nk=nk: nc.tensor.matmul(ps[ob][:, 0:W], lhsT=Vr[:, kc, vs[0]:vs[1]], rhs=Pt[:, 0:W], start=(j == 0), stop=(j == nk - 1))],
                                 reads=[bVr, bPt], writes=[psb[ob]])
                    yield None
                    if h % 2 == 0:
                        K.op(K.dve, [lambda ob=ob, W=W: nc.vector.reciprocal(out=rec[64:128, 0:W], in_=ps[ob][64:128, 0:W])], reads=[psb[ob]], writes=[brec])
                        K.op(K.dve, [lambda ob=ob, W=W, h=h, at=at: nc.vector.tensor_tensor(out=at[0:64, h // 2, 0:W], in0=ps[ob][0:64, 0:W], in1=rec[64:128, 0:W], op=ALU.mult)],
                             reads=[psb[ob], brec], writes=[bat])
                    else:
                        K.op(K.dve, [lambda ob=ob, W=W: nc.vector.reciprocal(out=rec[0:64, 0:W], in_=ps[ob][0:64, 0:W])], reads=[psb[ob]], writes=[brec])
                        K.op(K.dve, [lambda ob=ob, W=W, h=h, at=at: nc.vector.tensor_tensor(out=at[64:128, h // 2, 0:W], in0=ps[ob][64:128, 0:W], in1=rec[0:64, 0:W], op=ALU.mult)],
                             reads=[psb[ob], brec], writes=[bat])
                K.dma(K.qs, lambda e, at=at, c0=c0, W=W: e.dma_start(out=mixv[:, 2:4, c0:c0 + W], in_=at[:, :, 0:W]), reads=[bat], writes=[self.B["mixT"]])
            yield None

    def stage_sgp(self, l):
        nc, K, I, S = self.nc, self.K, self.I, self.S
        mixv = S["mixT"].rearrange("(c p) t -> p c t", p=128)
        uTv = S["uTd"].rearrange("(c p) t -> p c t", p=128)
        ps, psb = self.ps, self.psb
        with ExitStack() as st:
            wsT, bwsT = self.sb(st, "wsT", [128, 4, 128], BF16)
            bands, bbands = self.sb(st, "bands", [128, 20, 128], BF16)
            wpl, bwpl = self.sb(st, "wpl", [128, 2, 64], BF16)
            vr, bvr = self.sb(st, "vr", [128, 34, 256], BF16)
            pr, bpr = self.sb(st, "pr", [128, 34, 256], BF16)
            uts = [self.sb(st, f"ut{i}", [128, 2, 512], BF16) for i in range(2)]
            tmps_ = [self.sb(st, f"stmp{i}", [128, 2, 512], F32) for i in range(2)]
            pooleds = [self.sb(st, f"pooled{i}", [128, 2, 512], BF16) for i in range(2)]
            mts = [self.sb(st, f"mt{i}", [128, 4, 512], BF16) for i in range(2)]
            K.dma(K.qg, lambda e: e.dma_start(out=wsT[:, :, :], in_=I["wsT"][l].rearrange("q (h p) -> q h p", h=4)), writes=[bwsT])
            K.dma(K.qg, lambda e: e.dma_start(out=wpl[:, :, :], in_=I["wpl"][l].rearrange("r (g d) -> r g d", g=2)), writes=[bwpl])
            K.dma(K.qs, lambda e: e.dma_start(out=bands[:, :, :], in_=I["bands"].rearrange("p (v t) -> p v t", t=128)), writes=[bbands])
            K.dma(K.qs, lambda e: e.dma_start(out=vr[:, :, :], in_=S["vd"].rearrange("(k p) n -> p k n", p=128)), reads=[self.B["vd"]], writes=[bvr])
            K.dma(K.qs, lambda e: e.dma_start(out=pr[:, :, :], in_=S["pd"].rearrange("(k p) n -> p k n", p=128)), reads=[self.B["pd"]], writes=[bpr])
            tiles = [(t * 512, 512, 0, 31) for t in range(8)]
            if l == 0:
                tiles.append((L, 256, 32, 33))
            def sg_a(ti):
                c0, T, cfirst, clast = tiles[ti]
                nck = T // 128
                ut, but = uts[ti % 2]
                bo = 4 * (ti % 2)
                K.dma(K.qs, lambda e: e.dma_start(out=ut[:, :, 0:T], in_=uTv[:, :, c0:c0 + T]), reads=[self.B["uTd"]], writes=[but])
                for hp in range(2):
                    pb = bo + hp
                    fns = []
                    for ck in range(nck):
                        n = c0 // 128 + ck
                        for hh in range(2):
                            h = 2 * hp + hh
                            fns.append(lambda ck=ck, n=n, hh=hh, h=h, pb=pb: nc.tensor.matmul(ps[pb][hh * 64:(hh + 1) * 64, ck * 128:(ck + 1) * 128], lhsT=vr[:, n, h * 64:(h + 1) * 64], rhs=wsT[:, h, :], start=True, stop=True))
                    K.op(K.pe, fns, reads=[bvr, bwsT], writes=[psb[pb]])
                for gp in range(2):
                    pb = bo + 2 + gp
                    fns = []
                    for ck in range(nck):
                        n = c0 // 128 + ck
                        if n == cfirst:
                            srcs = [(n, 3), (n + 1, 2)]
                        elif n == clast:
                            srcs = [(n - 1, 0), (n, 4)]
                        else:
                            srcs = [(n - 1, 0), (n, 1), (n + 1, 2)]
                        for gg in range(2):
                            g_ = 2 * gp + gg
                            for si, (src, var) in enumerate(srcs):
                                fns.append(lambda ck=ck, gg=gg, g_=g_, src=src, var=var, si=si, ns=len(srcs), pb=pb: nc.tensor.matmul(
                                    ps[pb][gg * 64:(gg + 1) * 64, ck * 128:(ck + 1) * 128], lhsT=pr[:, src, g_ * 64:(g_ + 1) * 64], rhs=bands[:, g_ * 5 + var, :],
                                    start=(si == 0), stop=(si == ns - 1)))
                    K.op(K.pe, fns, reads=[bpr, bbands], writes=[psb[pb]])

            def sg_b(ti):
                c0, T, cfirst, clast = tiles[ti]
                nck = T // 128
                ut, but = uts[ti % 2]
                mt, bmt = mts[ti % 2]
                tmp, btmp = tmps_[ti % 2]
                pooled, bpooled = pooleds[ti % 2]
                bo = 4 * (ti % 2)
                for gp in range(2):
                    pb = bo + 2 + gp
                    K.op(K.act, [lambda gp=gp, pb=pb: nc.scalar.copy(out=pooled[:, gp, 0:T], in_=ps[pb][:, 0:T])], reads=[psb[pb]], writes=[bpooled])
                for hp in range(2):
                    pb = bo + hp
                    K.op(K.dve, [lambda pb=pb, hp=hp: nc.vector.tensor_tensor(out=tmp[:, hp, 0:T].rearrange("p (k c) -> p k c", c=128), in0=ps[pb][:, 0:T].rearrange("p (k c) -> p k c", c=128),
                                                                          in1=self.vcol(l, V_BSB + hp * 128, 128).unsqueeze(1).broadcast_to([128, nck, 128]), op=ALU.add)],
                         reads=[psb[pb], self.b_vecs], writes=[btmp])
                    K.op(K.dve, [lambda hp=hp: nc.vector.tensor_tensor(out=mt[:, hp, 0:T], in0=tmp[:, hp, 0:T], in1=ut[:, hp, 0:T], op=ALU.mult)],
                         reads=[btmp, but], writes=[bmt])
                for gp in range(2):
                    pb2 = bo + gp
                    fns = [(lambda gg=gg, gp=gp, pb2=pb2: nc.tensor.matmul(ps[pb2][gg * 64:(gg + 1) * 64, 0:T], lhsT=wpl[gg * 64:(gg + 1) * 64, gp, :], rhs=pooled[gg * 64:(gg + 1) * 64, gp, 0:T], start=True, stop=True))
                           for gg in range(2)]
                    K.op(K.pe, fns, reads=[bwpl, bpooled], writes=[psb[pb2]])
                    K.op(K.dve, [lambda gp=gp, pb2=pb2: nc.vector.tensor_scalar(out=mt[:, 2 + gp, 0:T], in0=ps[pb2][:, 0:T], scalar1=self.vcol(l, V_SP + gp), scalar2=None, op0=ALU.mult)],
                         reads=[psb[pb2], self.b_vecs], writes=[bmt])
                K.dma(K.qs, lambda e: e.dma_start(out=mixv[:, 4:8, c0:c0 + T], in_=mt[:, :, 0:T]), reads=[bmt], writes=[self.B["mixT"]])

            sg_a(0)
            for ti in range(len(tiles)):
                if ti + 1 < len(tiles):
                    sg_a(ti + 1)
                sg_b(ti)
            K.barrier()

    def stage_final(self):
        nc, K, S = self.nc, self.K, self.S
        hTv = S["hT"].rearrange("(c p) t -> p c t", p=128)
        with ExitStack() as st:
            self.ensure_eps(st)
            xts = [self.sb(st, f"fx{i}", [128, 8, 512], F32) for i in range(2)]
            sqs = [self.sb(st, f"fsq{i}", [128, 8, 512], BF16) for i in range(2)]
            rss = [self.sb(st, f"frs{i}", [128, 512], F32) for i in range(2)]
            ys = [self.sb(st, f"fy{i}", [128, 4, D], F32) for i in range(2)]
            gf = self.vecs[:, 2 * NVL:2 * NVL + 8]
            def fin_n(t):
                xt, bxt = xts[t % 2]
                yo, byo = ys[t % 2]
                sq, bsq = sqs[t % 2]
                rs, brs = rss[t % 2]
                K.dma(K.qs, lambda e, t=t, xt=xt: e.dma_start(out=xt[:, :, :], in_=hTv[:, :, t * 512:(t + 1) * 512]), reads=[self.B["hT"]], writes=[bxt])
                K.op(K.act, [lambda xt=xt, sq=sq: nc.scalar.activation(out=sq[:, :, :], in_=xt[:, :, :], func=AF.Square)], reads=[bxt], writes=[bsq])
                fns = [(lambda k=k, sq=sq: nc.tensor.matmul(self.ps[0][:, :], lhsT=self.ones[:, :], rhs=sq[:, k, :], start=(k == 0), stop=(k == 7))) for k in range(8)]
                K.op(K.pe, fns, reads=[bsq, self.b_ones], writes=[self.psb[0]])
                K.op(K.act, [lambda rs=rs: nc.scalar.activation(out=rs[:, :], in_=self.ps[0][:, :], func=AF.Sqrt, bias=self.epsb[:, 0:1], scale=1.0 / D)],
                     reads=[self.psb[0], self.b_eps], writes=[brs])
                K.op(K.dve, [lambda rs=rs: nc.vector.reciprocal(out=rs[:, :], in_=rs[:, :])], reads=[brs], writes=[brs])
                for c in range(8):
                    K.op(K.dve, [lambda c=c, xt=xt, rs=rs: nc.vector.scalar_tensor_tensor(
                        out=xt[:, c, :], in0=xt[:, c, :], scalar=gf[:, c:c + 1], in1=rs[:, :], op0=ALU.mult, op1=ALU.mult)],
                        reads=[bxt, brs, self.b_vecs], writes=[bxt])

            def fin_t(t):
                xt, bxt = xts[t % 2]
                yo, byo = ys[t % 2]
                for c in range(8):
                    pb = 1 + (c % 7)
                    fns = [(lambda k=k, c=c, pb=pb, xt=xt: nc.tensor.transpose(self.ps[pb][:, k * 128:(k + 1) * 128], xt[:, c, k * 128:(k + 1) * 128], self.ident[:, :]))
                           for k in range(4)]
                    K.op(K.pe, fns, reads=[bxt, self.b_ident], writes=[self.psb[pb]])
                    psv = self.ps[pb][:, :].rearrange("p (k d) -> p k d", d=128)
                    if c % 2 == 0:
                        K.op(K.act, [lambda c=c, psv=psv, yo=yo: nc.scalar.copy(out=yo[:, :, c * 128:(c + 1) * 128], in_=psv)], reads=[self.psb[pb]], writes=[byo])
                    else:
                        K.op(K.dve, [lambda c=c, psv=psv, yo=yo: nc.vector.tensor_copy(out=yo[:, :, c * 128:(c + 1) * 128], in_=psv)], reads=[self.psb[pb]], writes=[byo])
                outv = self.out[t * 512:(t + 1) * 512, :].rearrange("(k p) d -> p k d", p=128)
                K.dma(K.qs, lambda e, yo=yo, outv=outv: e.dma_start(out=outv, in_=yo[:, :, :]), reads=[byo], writes=[self.B["out"]])

            fin_n(0)
            for t in range(8):
                if t + 1 < 8:
                    fin_n(t + 1)
                fin_t(t)
            K.barrier()


def _rope_perm():
    p = np.zeros(32, np.int64)
    for a in range(2):
        for half in range(2):
            for j in range(8):
                p[a * 16 + half * 8 + j] = a * 16 + (1 - half) * 8 + j
    return p


def _constants():
    bf = ml_dtypes.bfloat16
    C = {}
    C["ident"] = np.eye(128, dtype=np.float32)
    cc = np.arange(64, dtype=np.float64)
    ang = 2 * np.pi * np.outer(cc, cc) / 64.0
    c64 = np.zeros((64, 4, 2, 128), np.float32)
    for kind, (f, Lx) in enumerate(((np.cos, L), (np.sin, L), (np.cos, CL), (np.sin, CL))):
        m = f(ang) / math.sqrt(64.0 * Lx)
        c64[:, kind, 0, 0:64] = m.T
        c64[:, kind, 1, 64:128] = m.T
    C["c64"] = c64.reshape(64, -1)
    t = np.arange(L)
    row = (t // 64).astype(np.float32)
    col = (t % 64).astype(np.float32)
    inv = np.power(np.float32(10000.0), -np.arange(0, 16, 2, dtype=np.float32) / np.float32(16)).astype(np.float32)
    rope = np.zeros((128, 2, NT), np.float32)
    rope[64:96, 0, L:] = 1.0
    for a, pos in enumerate((row, col)):
        angp = (pos[None, :] * inv[:, None]).astype(np.float32)
        cs, sn = np.cos(angp).astype(np.float32), np.sin(angp).astype(np.float32)
        for half in range(2):
            r0 = 64 + a * 16 + half * 8
            rope[r0:r0 + 8, 0, :L] = cs
            rope[r0:r0 + 8, 1, :L] = -sn if half == 0 else sn
    C["rope"] = rope.reshape(128, -1)
    bands = np.zeros((128, 4, 5, 128), np.float32)
    for gi, wdw in enumerate((2, 4, 8, 16)):
        for var in range(5):
            for tp in range(128):
                lo_rel, hi_rel = tp - wdw // 2, tp - wdw // 2 + wdw
                if var == 3:
                    lo_c, hi_c = max(lo_rel, 0), hi_rel
                elif var == 4:
                    lo_c, hi_c = lo_rel, min(hi_rel, 128)
                else:
                    lo_c, hi_c = lo_rel, hi_rel
                cnt = float(hi_c - lo_c)
                src_off = {0: -128, 1: 0, 2: 128, 3: 0, 4: 0}[var]
                for ts in range(128):
                    pos = ts + src_off
                    v = 0.0
                    if lo_c <= pos < hi_c:
                        v += 1.0 / cnt
                    if pos == tp:
                        v -= 1.0
                    bands[ts, gi, var, tp] = v
    C["bands"] = bands.reshape(128, -1).astype(bf)
    for R, nm1, nm3 in ((64, "fw1", "fm3"), (4, "fw1c", "fm3c")):
        Lx = 64 * R
        a1 = np.arange(R, dtype=np.float64)
        ang1 = 2 * np.pi * np.outer(a1, a1) / R
        Cw, Sw = np.cos(ang1), np.sin(ang1)
        W1 = np.zeros((2 * R, 2 * R), np.float64)
        W1[0:R, 0:R] = Cw
        W1[R:, 0:R] = -Sw
        W1[0:R, R:] = -Sw
        W1[R:, R:] = -Cw
        C[nm1] = W1.astype(np.float32).astype(bf)
        l2 = np.arange(64, dtype=np.float64)[:, None, None]
        l1p = np.arange(R, dtype=np.float64)[None, :, None]
        l2p = np.arange(64, dtype=np.float64)[None, None, :]
        ang3 = 2 * np.pi * l2 * (l1p + R * l2p) / Lx
        M3 = np.concatenate([np.cos(ang3), np.sin(ang3)], axis=0)
        C[nm3] = M3.reshape(128, R * 64).astype(np.float32).astype(bf)
    return C


_CONST = None


def _prep_inputs(inp):
    global _CONST
    if _CONST is None:
        _CONST = _constants()
    f = lambda a: np.ascontiguousarray(np.asarray(a, dtype=np.float32))
    sh = {}
    sh["wada"] = f(inp["w_ada"])
    sh["w13a"] = f(inp["w13_ffn1"]); sh["w2a"] = f(inp["w2_ffn1"])
    sh["w13b"] = f(inp["w13_ffn2"]); sh["w2b"] = f(inp["w2_ffn2"])
    w_in = f(inp["w_in"])
    perm = _rope_perm()
    winx = np.zeros((2, D, 1536), np.float32)
    winx[:, :, 0:256] = w_in[:, :, 0:256]
    winx[:, :, 256:384] = w_in[:, :, 256:384]
    winx[:, :, 384:448] = w_in[:, :, 384:448]
    winx[:, :, 448:480] = w_in[:, :, 576:608]
    winx[:, :, 512:640] = w_in[:, :, 448:576]
    winx[:, :, 640 + 64:640 + 96] = w_in[:, :, 576:608][:, :, perm]
    winx[:, :, 768:1024] = w_in[:, :, 608:864]
    winx[:, :, 1024:1536] = w_in[:, :, 864:1376]
    sh["winx"] = winx
    w_uq = f(inp["w_uq"])
    wuqp = w_uq.copy()
    for h in range(4):
        wuqp[:, :, h * 96 + 64:h * 96 + 96] = w_uq[:, :, h * 96 + 64:h * 96 + 96][:, :, perm]
    sh["wuq"] = w_uq; sh["wuqp"] = wuqp
    sh["wukv"] = f(inp["w_ukv"])
    sh["wfn"] = np.ascontiguousarray(f(inp["w_fnet"]).transpose(0, 2, 1, 3).reshape(2, 64, 256))
    sh["wsT"] = np.ascontiguousarray(f(inp["w_sgu"]).transpose(0, 3, 1, 2).reshape(2, 128, 512))
    wp = f(inp["w_pool"])
    wpl = np.zeros((2, 128, 2, 64), np.float32)
    for g in range(4):
        wpl[:, (g % 2) * 64:(g % 2) * 64 + 64, g // 2, :] = wp[:, g]
    sh["wpl"] = wpl.reshape(2, 128, 128)
    sh["wout"] = f(inp["w_out"])
    vecs = np.zeros((128, 2 * NVL + 8), np.float32)
    fm = lambda v: np.asarray(v, np.float32).reshape(-1, 128).T
    for l in range(2):
        o = l * NVL
        vecs[:, o + V_GF1:o + V_GF1 + 8] = fm(inp["g_ffn1"][l])
        vecs[:, o + V_GMIX:o + V_GMIX + 8] = fm(inp["g_mix"][l])
        vecs[:, o + V_GF2:o + V_GF2 + 8] = fm(inp["g_ffn2"][l])
        vecs[:, o + V_BADA:o + V_BADA + 72] = fm(inp["b_ada"][l])
        gq = np.asarray(inp["g_q"][l], np.float32)
        vecs[:, o + V_GQ0] = gq[0:128]
        vecs[0:64, o + V_GQ1] = gq[128:192]
        vecs[:, o + V_GKV] = np.asarray(inp["g_kv"][l], np.float32)
        vecs[:, o + V_SP:o + V_SP + 2] = fm(inp["s_pool"][l])
        bs = np.asarray(inp["b_sgu"][l], np.float32)
        bsb = np.zeros((128, 2, 128), np.float32)
        for h in range(4):
            bsb[(h % 2) * 64:(h % 2) * 64 + 64, h // 2, :] = bs[h][None, :]
        vecs[:, o + V_BSB:o + V_BSB + 256] = bsb.reshape(128, 256)
        vecs[:, o + V_GSB:o + V_GSB + 256] = np.asarray(inp["g_sgu"][l], np.float32).reshape(1, 256)
    vecs[:, 2 * NVL:2 * NVL + 8] = fm(inp["g_final"])
    sh["vecs"] = vecs
    sh.update(_CONST)
    x = f(inp["x"]); ctx = f(inp["ctx"]); c = f(inp["c"]); c_ctx = f(inp["c_ctx"])
    maps = []
    for b in range(x.shape[0]):
        m = dict(sh)
        m["x"] = x[b]
        m["ctx"] = ctx[b]
        cc = np.zeros((128, 8, 2), np.float32)
        cc[:, :, 0] = c[b].reshape(8, 128).T
        cc[:, :, 1] = c_ctx.reshape(8, 128).T
        m["cc"] = cc.reshape(128, 16)
        maps.append(m)
    return maps


_NC_CACHE = {}


def kernel(**inputs):
    maps = _prep_inputs(inputs)
    if "full" not in _NC_CACHE:
        _NC_CACHE["full"] = Prog({}).build()
    nc = _NC_CACHE["full"]
    res = run_bass_kernel_spmd(nc, maps, core_ids=list(range(len(maps))))
    return np.stack([np.asarray(r["out"], dtype=np.float32) for r in res.results], axis=0)
```
